# Optimizing a Trainium2 kernel written in Bass

```python
import math
import jax
import jax.numpy as jnp
from jax import lax
import numpy as np

D_MODEL = 1024
BATCH = 4
SEQ = 4096
DEPTH = 4

GRID_W = 64
CTX_LEN = 256
EPS = 1e-6
N_BRANCH = 3
BRANCH_W = D_MODEL // 2
N_MOD = 9
D_FF = ((8 * D_MODEL // 3 + 255) // 256) * 256
CONV_W = 4
CONV_LEFT = 2
ML_HEADS = 4
ML_DV = BRANCH_W // ML_HEADS
ML_DK = ML_DV // 2
ML_QK = ML_HEADS * ML_DK
ML_V = BRANCH_W
ML_CHUNK = 64
LRU_WIDTH = BRANCH_W
LRU_BLOCKS = 8
LRU_BLOCK = LRU_WIDTH // LRU_BLOCKS
LRU_C = 8.0
DN_HEADS = 4
DN_DV = BRANCH_W // DN_HEADS
DN_DK = DN_DV
DN_K = DN_HEADS * DN_DK
DN_V = BRANCH_W
DN_CHUNK = 64
IN_WIDTHS = (ML_QK, ML_QK, ML_V, ML_V, 4 * ML_HEADS,
             LRU_WIDTH, LRU_WIDTH,
             DN_K, DN_K, DN_V, DN_V, 4 * DN_HEADS,
             N_BRANCH * D_MODEL)
D_IN = sum(IN_WIDTHS)

kernel_name = "hybrid_mlstm_rglru_gdn_prefix_dit"


def rmsnorm(x, g):
    x32 = x.astype(jnp.float32)
    y = x32 * lax.rsqrt(jnp.mean(x32 * x32, axis=-1, keepdims=True) + EPS)
    return (y * g.astype(jnp.float32)).astype(x.dtype)


def l2norm(x):
    x32 = x.astype(jnp.float32)
    return (x32 * lax.rsqrt(jnp.sum(x32 * x32, axis=-1, keepdims=True) + EPS)).astype(x.dtype)


def modulate(x, g, shift, scale):
    return rmsnorm(x, g) * (1 + scale) + shift


def centred_conv(x, w):
    L = x.shape[1]
    xp = jnp.pad(x, ((0, 0), (CONV_LEFT, CONV_W - 1 - CONV_LEFT), (0, 0)))
    return sum(xp[:, j:j + L] * w[j] for j in range(CONV_W))


def to_chunks(a, T):
    B, L = a.shape[:2]
    return jnp.moveaxis(a.reshape((B, L // T, T) + a.shape[2:]), 1, 0)


def from_chunks(a):
    nc, B, T = a.shape[:3]
    return jnp.moveaxis(a, 0, 1).reshape((B, nc * T) + a.shape[3:])


def to_colmajor(a):
    B, S = a.shape[:2]
    rows = S // GRID_W
    a = a.reshape((B, rows, GRID_W) + a.shape[2:])
    return jnp.swapaxes(a, 1, 2).reshape((B, S) + a.shape[3:])


def from_colmajor(a):
    B, S = a.shape[:2]
    rows = S // GRID_W
    a = a.reshape((B, GRID_W, rows) + a.shape[2:])
    return jnp.swapaxes(a, 1, 2).reshape((B, S) + a.shape[3:])


def bidir_prefix(scan_fn, ctx_f, ctx_b, lat_f, lat_b, init):
    flip = lambda t: tuple(jnp.flip(a, axis=1) for a in t)
    st_f, yc_f = scan_fn(ctx_f, init)
    _, yl_f = scan_fn(lat_f, st_f)
    st_b, yc_b = scan_fn(flip(ctx_b), init)
    _, yl_b = scan_fn(flip(lat_b), st_b)
    return yc_f + jnp.flip(yc_b, axis=1), yl_f + jnp.flip(yl_b, axis=1)


def mlstm_scan(inp, state):
    dtype = inp[0].dtype
    q, k, v, ig, fg = (a.astype(jnp.float32) for a in inp)
    T = ML_CHUNK
    causal = jnp.tril(jnp.ones((T, T), dtype=bool))

    def step(carry, blk):
        C, n, m = carry
        qc, kc, vc, ic, fc = blk
        b = jnp.cumsum(jax.nn.log_sigmoid(fc), axis=1)
        logD = b[:, :, None, :] - b[:, None, :, :] + ic[:, None, :, :]
        logD = jnp.where(causal[None, :, :, None], logD, -jnp.inf)
        inter = b + m[:, None, :]
        m_t = jnp.maximum(inter, jnp.max(logD, axis=2))
        w_in = jnp.exp(inter - m_t)
        s = jnp.einsum('bthd,bshd->btsh', qc, kc) * jnp.exp(logD - m_t[:, :, None, :])
        num = jnp.einsum('btsh,bshv->bthv', s, vc) + w_in[..., None] * jnp.einsum('bthd,bhdv->bthv', qc, C)
        den = jnp.sum(s, axis=2) + w_in * jnp.einsum('bthd,bhd->bth', qc, n)
        h = num / jnp.maximum(jnp.abs(den), jnp.exp(-m_t))[..., None]
        bT = b[:, -1]
        log_ws = bT[:, None, :] - b + ic
        m_new = jnp.maximum(bT + m, jnp.max(log_ws, axis=1))
        ws = jnp.exp(log_ws - m_new[:, None, :])
        dec = jnp.exp(bT + m - m_new)
        C = dec[..., None, None] * C + jnp.einsum('bth,bthd,bthv->bhdv', ws, kc, vc)
        n = dec[..., None] * n + jnp.einsum('bth,bthd->bhd', ws, kc)
        return (C, n, m_new), h

    state, h = lax.scan(step, state, tuple(to_chunks(a, T) for a in (q, k, v, ig, fg)))
    return state, from_chunks(h).astype(dtype)


def _mlstm_prep(q, k, v, gt, gate_b):
    B, L, _ = q.shape
    q = q.reshape(B, L, ML_HEADS, ML_DK) * (ML_DK ** -0.5)
    k = k.reshape(B, L, ML_HEADS, ML_DK)
    v = v.reshape(B, L, ML_HEADS, ML_DV)
    gt = gt.reshape(B, L, 4, ML_HEADS) + gate_b
    return (q, k, v, gt[:, :, 0], gt[:, :, 2]), (q, k, v, gt[:, :, 1], gt[:, :, 3])


def _mlstm_out(h, o, norm_g):
    B, L = h.shape[:2]
    h = rmsnorm(h, norm_g.reshape(ML_HEADS, ML_DV)).reshape(B, L, ML_V)
    return h * jax.nn.sigmoid(o)


def mlstm_branch(pc, pl, gate_b, norm_g, with_ctx_out):
    fc, bc = _mlstm_prep(pc[0], pc[1], pc[2], pc[4], gate_b)
    fl, bl = _mlstm_prep(pl[0], pl[1], pl[2], pl[4], gate_b)
    B = pl[0].shape[0]
    init = (jnp.zeros((B, ML_HEADS, ML_DK, ML_DV), jnp.float32),
            jnp.zeros((B, ML_HEADS, ML_DK), jnp.float32),
            jnp.zeros((B, ML_HEADS), jnp.float32))
    hc, hl = bidir_prefix(mlstm_scan, fc, bc, fl, bl, init)
    yc = _mlstm_out(hc, pc[3], norm_g) if with_ctx_out else None
    return yc, _mlstm_out(hl, pl[3], norm_g)


def blockdiag(x, w):
    B, L, C = x.shape
    return jnp.einsum('blni,nij->blnj', x.reshape(B, L, LRU_BLOCKS, LRU_BLOCK), w).reshape(B, L, C)


def lru_scan(inp, h0):
    dtype = inp[1].dtype
    log_a, bx = (a.astype(jnp.float32) for a in inp)
    a = jnp.exp(log_a)
    bx = bx.at[:, 0].add(a[:, 0] * h0)
    comb = lambda l, r: (l[0] * r[0], r[0] * l[1] + r[1])
    _, h = lax.associative_scan(comb, (a, bx), axis=1)
    return h[:, -1], h.astype(dtype)


def _lru_prep(xb, conv_w, conv_b, w_a, b_a, w_x, b_x, lam):
    xc = centred_conv(xb, conv_w) + conv_b

    def direction(d):
        r = jax.nn.sigmoid(blockdiag(xc, w_a[d]) + b_a[d])
        i = jax.nn.sigmoid(blockdiag(xc, w_x[d]) + b_x[d])
        log_a = -LRU_C * jax.nn.softplus(-lam[d]) * r
        return (log_a, jnp.sqrt(-jnp.expm1(2 * log_a)) * (i * xc))

    return direction(0), direction(1)


def lru_branch(pc, pl, conv_w, conv_b, w_a, b_a, w_x, b_x, lam, with_ctx_out):
    fc, bc = _lru_prep(pc[0], conv_w, conv_b, w_a, b_a, w_x, b_x, lam)
    fl, bl = _lru_prep(pl[0], conv_w, conv_b, w_a, b_a, w_x, b_x, lam)
    init = jnp.zeros((pl[0].shape[0], LRU_WIDTH), jnp.float32)
    hc, hl = bidir_prefix(lru_scan, fc, bc, fl, bl, init)
    yc = hc * jax.nn.gelu(pc[1]) if with_ctx_out else None
    return yc, hl * jax.nn.gelu(pl[1])


def gdn_scan(inp, S0):
    dtype = inp[0].dtype
    q, k, v, g, beta = (a.astype(jnp.float32) for a in inp)
    T = DN_CHUNK
    incl = jnp.tril(jnp.ones((T, T), dtype=bool))
    strict = jnp.tril(jnp.ones((T, T), dtype=bool), -1)
    eye = jnp.eye(T, dtype=jnp.float32)

    def step(S, blk):
        qc, kc, vc, gc, bc = blk
        qc, kc, vc = (jnp.swapaxes(a, 1, 2) for a in (qc, kc, vc))
        gc, bc = jnp.swapaxes(gc, 1, 2), jnp.swapaxes(bc, 1, 2)
        G = jnp.cumsum(gc, axis=-1)
        diff = G[..., :, None] - G[..., None, :]
        gam = jnp.where(incl, jnp.exp(jnp.where(incl, diff, 0.0)), 0.0)
        A = jnp.where(strict, bc[..., :, None] * jnp.einsum('bhtd,bhsd->bhts', kc, kc) * gam, 0.0)
        rhs = jnp.concatenate([bc[..., None] * vc, bc[..., None] * kc * jnp.exp(G)[..., None]], axis=-1)
        sol = lax.linalg.triangular_solve(eye + A, rhs, left_side=True, lower=True, unit_diagonal=True)
        u, w = sol[..., :DN_DV], sol[..., DN_DV:]
        vnew = u - jnp.einsum('bhtk,bhkv->bhtv', w, S)
        o = (jnp.exp(G)[..., None] * jnp.einsum('bhtk,bhkv->bhtv', qc, S)
             + jnp.einsum('bhts,bhsv->bhtv', jnp.einsum('bhtk,bhsk->bhts', qc, kc) * gam, vnew))
        GT = G[..., -1]
        S = (jnp.exp(GT)[..., None, None] * S
             + jnp.einsum('bhs,bhsk,bhsv->bhkv', jnp.exp(GT[..., None] - G), kc, vnew))
        return S, jnp.swapaxes(o, 1, 2)

    S, o = lax.scan(step, S0, tuple(to_chunks(a, T) for a in (q, k, v, g, beta)))
    return S, from_chunks(o).astype(dtype)


def _gdn_prep(q, k, v, ba, conv_w, a_log, dt_bias):
    B, L, _ = q.shape
    qkv = jax.nn.silu(centred_conv(jnp.concatenate([q, k, v], axis=-1), conv_w))
    q, k, v = jnp.split(qkv, [DN_K, 2 * DN_K], axis=-1)
    q = l2norm(q.reshape(B, L, DN_HEADS, DN_DK)) * (DN_DK ** -0.5)
    k = l2norm(k.reshape(B, L, DN_HEADS, DN_DK))
    v = v.reshape(B, L, DN_HEADS, DN_DV)
    ba = ba.reshape(B, L, 4, DN_HEADS).astype(jnp.float32)
    beta = jax.nn.sigmoid(ba[:, :, 0:2])
    g = -jnp.exp(a_log.astype(jnp.float32)) * jax.nn.softplus(ba[:, :, 2:4] + dt_bias.astype(jnp.float32))
    return (q, k, v, g[:, :, 0], beta[:, :, 0]), (q, k, v, g[:, :, 1], beta[:, :, 1])


def _gdn_out(o, z, norm_g):
    B, L = o.shape[:2]
    z = z.reshape(B, L, DN_HEADS, DN_DV)
    return (rmsnorm(o, norm_g) * jax.nn.silu(z)).reshape(B, L, DN_V)


def dn_branch(pc, pl, conv_w, a_log, dt_bias, norm_g, with_ctx_out):
    fc, bc = _gdn_prep(pc[0], pc[1], pc[2], pc[4], conv_w, a_log, dt_bias)
    fl, bl = _gdn_prep(to_colmajor(pl[0]), to_colmajor(pl[1]), to_colmajor(pl[2]), to_colmajor(pl[4]),
                       conv_w, a_log, dt_bias)
    init = jnp.zeros((pl[0].shape[0], DN_HEADS, DN_DK, DN_DV), jnp.float32)
    oc, ol = bidir_prefix(gdn_scan, fc, bc, fl, bl, init)
    yc = _gdn_out(oc, pc[3], norm_g) if with_ctx_out else None
    return yc, _gdn_out(from_colmajor(ol), pl[3], norm_g)


def _split_in(p):
    idx = np.cumsum(IN_WIDTHS)[:-1].tolist()
    return jnp.split(p, idx, axis=-1)


def merge(ys, gate_cols, w_branch, w_out):
    B, L, _ = gate_cols.shape
    gates = jax.nn.sigmoid(gate_cols.reshape(B, L, N_BRANCH, D_MODEL))
    proj = jnp.einsum('blnc,ncd->blnd', jnp.stack(ys, axis=2), w_branch)
    return jnp.sum(gates * proj, axis=2) @ w_out


def mixer(hc, hl, w_in, ml_gate_b, ml_norm_g, lru_conv_w, lru_conv_b, lru_w_a, lru_b_a, lru_w_x, lru_b_x,
          lru_lambda, dn_conv_w, dn_a_log, dn_dt_bias, dn_norm_g, w_branch, w_out, with_ctx_out):
    sc = _split_in(hc @ w_in)
    sl = _split_in(hl @ w_in)
    ml_c, ml_l = mlstm_branch(sc[0:5], sl[0:5], ml_gate_b, ml_norm_g, with_ctx_out)
    lr_c, lr_l = lru_branch(sc[5:7], sl[5:7], lru_conv_w, lru_conv_b, lru_w_a, lru_b_a, lru_w_x, lru_b_x,
                            lru_lambda, with_ctx_out)
    dn_c, dn_l = dn_branch(sc[7:12], sl[7:12], dn_conv_w, dn_a_log, dn_dt_bias, dn_norm_g, with_ctx_out)
    yl = merge((ml_l, lr_l, dn_l), sl[12], w_branch, w_out)
    yc = merge((ml_c, lr_c, dn_c), sc[12], w_branch, w_out) if with_ctx_out else None
    return yc, yl


def ffn_sub(x, mod, j, g_pre, g_post, w_gu, w_down):
    h = modulate(x, g_pre, mod[:, 3 * j], mod[:, 3 * j + 1])
    gu = h @ w_gu
    y = (jax.nn.silu(gu[..., :D_FF]) * gu[..., D_FF:]) @ w_down
    return x + 0.5 * mod[:, 3 * j + 2] * rmsnorm(y, g_post)


def setup_inputs(seed: int = 0) -> dict:
    key = jax.random.key(seed)
    ks = jax.random.split(key, 32)
    f32 = jnp.float32
    nrm = lambda i, shape, s: jax.random.normal(ks[i], shape, f32) * s
    x = nrm(0, (BATCH, SEQ, D_MODEL), 1.0)
    c = nrm(1, (BATCH, D_MODEL), 1.0)
    ctx = nrm(2, (BATCH, CTX_LEN, D_MODEL), 1.0)
    c_ctx = nrm(3, (D_MODEL,), 1.0)
    w_mod = nrm(4, (DEPTH, D_MODEL, N_MOD * D_MODEL), D_MODEL ** -0.5)
    b_mod = nrm(5, (DEPTH, N_MOD * D_MODEL), 0.02)
    norm_g = 1.0 + nrm(6, (DEPTH, 6, D_MODEL), 0.02)
    ffn_w_gu = nrm(7, (DEPTH, 2, D_MODEL, 2 * D_FF), D_MODEL ** -0.5)
    ffn_w_down = nrm(8, (DEPTH, 2, D_FF, D_MODEL), D_FF ** -0.5)
    w_in = nrm(9, (DEPTH, D_MODEL, D_IN), D_MODEL ** -0.5)
    ig_b = -2.0 + nrm(10, (DEPTH, 2, ML_HEADS), 0.1)
    fg_b = jnp.linspace(3.0, 6.0, ML_HEADS, dtype=f32) + nrm(11, (DEPTH, 2, ML_HEADS), 0.1)
    ml_gate_b = jnp.concatenate([ig_b, fg_b], axis=1)
    ml_norm_g = 1.0 + nrm(12, (DEPTH, ML_V), 0.02)
    lru_conv_w = nrm(13, (DEPTH, CONV_W, LRU_WIDTH), CONV_W ** -0.5)
    lru_conv_b = nrm(14, (DEPTH, LRU_WIDTH), 0.02)
    lru_w_a = nrm(15, (DEPTH, 2, LRU_BLOCKS, LRU_BLOCK, LRU_BLOCK), LRU_BLOCK ** -0.5)
    lru_b_a = nrm(16, (DEPTH, 2, LRU_WIDTH), 0.02)
    lru_w_x = nrm(17, (DEPTH, 2, LRU_BLOCKS, LRU_BLOCK, LRU_BLOCK), LRU_BLOCK ** -0.5)
    lru_b_x = nrm(18, (DEPTH, 2, LRU_WIDTH), 0.02)
    u = jax.random.uniform(ks[19], (DEPTH, 2, LRU_WIDTH), f32, 0.9, 0.999)
    sig = u ** (1.0 / LRU_C)
    lru_lambda = jnp.log(sig) - jnp.log1p(-sig)
    dn_conv_w = nrm(20, (DEPTH, CONV_W, 2 * DN_K + DN_V), CONV_W ** -0.5)
    dn_a_log = jnp.log(jax.random.uniform(ks[21], (DEPTH, 2, DN_HEADS), f32, 1.0, 16.0))
    dt = jnp.exp(jax.random.uniform(ks[22], (DEPTH, 2, DN_HEADS), f32, math.log(1e-3), math.log(0.1)))
    dn_dt_bias = dt + jnp.log(-jnp.expm1(-dt))
    dn_norm_g = 1.0 + nrm(23, (DEPTH, DN_DV), 0.02)
    w_branch = nrm(24, (DEPTH, N_BRANCH, BRANCH_W, D_MODEL), BRANCH_W ** -0.5)
    w_out = nrm(25, (DEPTH, D_MODEL, D_MODEL), D_MODEL ** -0.5)
    return {"x": x, "c": c, "ctx": ctx, "c_ctx": c_ctx, "w_mod": w_mod, "b_mod": b_mod,
            "norm_g": norm_g, "ffn_w_gu": ffn_w_gu, "ffn_w_down": ffn_w_down, "w_in": w_in,
            "ml_gate_b": ml_gate_b, "ml_norm_g": ml_norm_g, "lru_conv_w": lru_conv_w,
            "lru_conv_b": lru_conv_b, "lru_w_a": lru_w_a, "lru_b_a": lru_b_a, "lru_w_x": lru_w_x,
            "lru_b_x": lru_b_x, "lru_lambda": lru_lambda, "dn_conv_w": dn_conv_w,
            "dn_a_log": dn_a_log, "dn_dt_bias": dn_dt_bias, "dn_norm_g": dn_norm_g,
            "w_branch": w_branch, "w_out": w_out}


def reference(x, c, ctx, c_ctx, w_mod, b_mod, norm_g, ffn_w_gu, ffn_w_down, w_in, ml_gate_b, ml_norm_g,
              lru_conv_w, lru_conv_b, lru_w_a, lru_b_a, lru_w_x, lru_b_x, lru_lambda, dn_conv_w,
              dn_a_log, dn_dt_bias, dn_norm_g, w_branch, w_out):
    B = x.shape[0]
    z = ctx
    for l in range(DEPTH):
        last = l == DEPTH - 1
        mod_l = (jax.nn.silu(c) @ w_mod[l] + b_mod[l]).reshape(B, N_MOD, 1, D_MODEL)
        mod_c = (jax.nn.silu(c_ctx) @ w_mod[l] + b_mod[l]).reshape(1, N_MOD, 1, D_MODEL)
        x = ffn_sub(x, mod_l, 0, norm_g[l, 0], norm_g[l, 1], ffn_w_gu[l, 0], ffn_w_down[l, 0])
        z = ffn_sub(z, mod_c, 0, norm_g[l, 0], norm_g[l, 1], ffn_w_gu[l, 0], ffn_w_down[l, 0])
        hl = modulate(x, norm_g[l, 2], mod_l[:, 3], mod_l[:, 4])
        hc = modulate(z, norm_g[l, 2], mod_c[:, 3], mod_c[:, 4])
        yc, yl = mixer(hc, hl, w_in[l], ml_gate_b[l], ml_norm_g[l], lru_conv_w[l], lru_conv_b[l],
                       lru_w_a[l], lru_b_a[l], lru_w_x[l], lru_b_x[l], lru_lambda[l], dn_conv_w[l],
                       dn_a_log[l], dn_dt_bias[l], dn_norm_g[l], w_branch[l], w_out[l], not last)
        x = x + mod_l[:, 5] * rmsnorm(yl, norm_g[l, 3])
        x = ffn_sub(x, mod_l, 2, norm_g[l, 4], norm_g[l, 5], ffn_w_gu[l, 1], ffn_w_down[l, 1])
        if not last:
            z = z + mod_c[:, 5] * rmsnorm(yc, norm_g[l, 3])
            z = ffn_sub(z, mod_c, 2, norm_g[l, 4], norm_g[l, 5], ffn_w_gu[l, 1], ffn_w_down[l, 1])
    return x
```

```python
import contextlib
import numpy as np
import concourse.bass as bass
import concourse.mybir as mybir
from concourse.bass_utils import run_bass_kernel_spmd

F32 = mybir.dt.float32
BF16 = mybir.dt.bfloat16
ALU = mybir.AluOpType
AF = mybir.ActivationFunctionType

ENGS = ('pe', 'act', 'dve', 'pool', 'sp')
NDMASLOT = {'sp': 48, 'pool': 16, 'act': 16, 'pe': 1, 'dve': 1}
EPOCH = 20000

D = 1024
LAT = 4096
CTX = 256
NTOK = LAT + CTX
DFF = 2816
DIN = 7712
DEPTH = 4
EPS = 1e-6
NTILE = NTOK // 128


class Buf:
    __slots__ = ('name', 'w', 'r', 'excl')

    def __init__(self, name='', excl=False):
        self.name = name
        self.w = None
        self.r = []
        self.excl = excl


class Prog:
    def __init__(self, nc):
        self.nc = nc
        self.es = contextlib.ExitStack()
        self.ins = {e: [] for e in ENGS}
        self.seen = {e: {f: -1 for f in ENGS} for e in ENGS}
        self.seen_dma = {e: set() for e in ENGS}
        self.ndma = {e: 0 for e in ENGS}
        self.dma_list = {e: [] for e in ENGS}
        self.nbuf = 0
        self.ntens = 0

    def sb(self, name, shape, dtype, stack=None):
        self.ntens += 1
        return (stack or self.es).enter_context(self.nc.sbuf_tensor(f'{name}_{self.ntens}', list(shape), dtype))

    def ps(self, name, shape, dtype):
        return self.es.enter_context(self.nc.psum_tensor(name, list(shape), dtype))

    def buf(self, name='', excl=False):
        self.nbuf += 1
        return Buf(name or f'b{self.nbuf}', excl)

    def op(self, eng, fn, reads=(), writes=(), dma=False):
        lst = self.ins[eng]
        idx = len(lst)
        deps = {}
        dma_deps = set()

        def need(key):
            e, i = key
            if self.ins[e][i]['dma']:
                dma_deps.add((e, i))
            elif i > deps.get(e, -1):
                deps[e] = i
        for b in reads:
            if b.w is not None:
                need(b.w)
            if b.excl:
                for k in b.r:
                    if k[0] != eng:
                        need(k)
        for b in writes:
            if b.w is not None:
                need(b.w)
            for k in b.r:
                need(k)
        waits = []
        for e, i in deps.items():
            if e == eng and eng == 'pe':
                continue
            if i <= self.seen[eng][e]:
                continue
            self.seen[eng][e] = i
            waits.append((e, i))
            self.ins[e][i]['sig'] = True
        dwaits = []
        for k in dma_deps:
            if k in self.seen_dma[eng]:
                continue
            self.seen_dma[eng].add(k)
            dwaits.append(k)
        rec = dict(fn=fn, waits=waits, dwaits=dwaits, dma=dma, sig=False, gid=None)
        if dma:
            gid = self.ndma[eng]
            self.ndma[eng] += 1
            rec['gid'] = gid
            self.dma_list[eng].append((eng, idx))
            if gid >= NDMASLOT[eng]:
                prev = self.dma_list[eng][gid - NDMASLOT[eng]]
                if prev not in self.seen_dma[eng]:
                    self.seen_dma[eng].add(prev)
                    rec['dwaits'].append(prev)
        lst.append(rec)
        key = (eng, idx)
        for b in reads:
            b.r.append(key)
        for b in writes:
            b.w = key
            b.r = []
        return key

    def I(self, eng, name, *args, reads=(), writes=(), **kw):
        return self.op(eng, (name, args, kw), reads, writes)

    def dma(self, eng, out, in_, reads=(), writes=(), **kw):
        return self.op(eng, ('dma_start', (), dict(out=out, in_=in_, **kw)), reads, writes, dma=True)

    def mm(self, out, lhsT, rhs, start, stop, reads=(), writes=()):
        return self.op('pe', ('matmul', (out, lhsT, rhs), dict(start=start, stop=stop)), reads, writes)

    def wait(self, eng, reads):
        return self.op(eng, None, reads, ())

    def barrier(self):
        lastc = {}
        for f in ENGS:
            for i in range(len(self.ins[f]) - 1, -1, -1):
                r = self.ins[f][i]
                if (not r['dma']) and r['fn'] is not None:
                    lastc[f] = i
                    break
        recent = []
        for f in ENGS:
            recent += self.dma_list[f][max(0, self.ndma[f] - NDMASLOT[f]):]
        for e in ENGS:
            waits = []
            for f, i in lastc.items():
                if f == e or i <= self.seen[e][f]:
                    continue
                self.seen[e][f] = i
                waits.append((f, i))
                self.ins[f][i]['sig'] = True
            dwaits = []
            for k in recent:
                if k in self.seen_dma[e]:
                    continue
                self.seen_dma[e].add(k)
                dwaits.append(k)
            self.ins[e].append(dict(fn=None, waits=waits, dwaits=dwaits, dma=False, sig=False, gid=None))

    def emit(self):
        nc = self.nc
        nsig = {}
        for e in ENGS:
            r = 0
            for rec in self.ins[e]:
                if rec['dma']:
                    continue
                if rec['sig']:
                    rec['rank'] = r
                    r += 1
            nsig[e] = r
        sems = {}
        for e in ENGS:
            n_ep = (nsig[e] + EPOCH - 1) // EPOCH
            sems[e] = [self.es.enter_context(nc.semaphore(f'sem_{e}_{k}')) for k in range(max(n_ep, 1))]
        dsem = {f: [self.es.enter_context(nc.semaphore(f'dsem_{f}_{k}')) for k in range(NDMASLOT[f])]
                for f in ENGS if self.ndma[f] > 0}

        def run(e, eo):
            for rec in self.ins[e]:
                for (f, i) in rec['waits']:
                    rk = self.ins[f][i]['rank']
                    eo.wait_ge(sems[f][rk // EPOCH], rk % EPOCH + 1)
                for (f, i) in rec['dwaits']:
                    gid = self.ins[f][i]['gid']
                    eo.wait_ge(dsem[f][gid % NDMASLOT[f]], 16 * (gid // NDMASLOT[f] + 1))
                if rec['fn'] is None:
                    continue
                name, args, kw = rec['fn']
                inst = getattr(eo, name)(*args, **kw)
                if rec['dma']:
                    gid = rec['gid']
                    inst.then_inc(dsem[e][gid % NDMASLOT[e]], 16)
                elif rec['sig']:
                    rk = rec['rank']
                    inst.then_inc(sems[e][rk // EPOCH], 1)
        block = self.es.enter_context(nc.Block())

        @block.tensor
        def _(eo):
            run('pe', eo)

        @block.scalar
        def _(eo):
            run('act', eo)

        @block.vector
        def _(eo):
            run('dve', eo)

        @block.gpsimd
        def _(eo):
            run('pool', eo)

        @block.sync
        def _(eo):
            run('sp', eo)

    def close(self):
        self.es.close()


class Ring:
    def __init__(self, P, name, n, shape, dtype, stack):
        self.slots = [(P.sb(f'{name}{i}', shape, dtype, stack), P.buf(f'{name}{i}')) for i in range(n)]
        self.i = 0

    def next(self):
        s = self.slots[self.i % len(self.slots)]
        self.i += 1
        return s


def bcast_free(ap, n):
    a = [list(q) for q in ap.ap]
    return bass.AP(ap.tensor, ap.offset, [a[0], [0, n]])


def bcast_part(dram_ap_1d, nparts=128):
    a = [list(q) for q in dram_ap_1d.ap]
    return bass.AP(dram_ap_1d.tensor, dram_ap_1d.offset, [[0, nparts]] + a)


def rev_free(ap):
    a = [list(q) for q in ap.ap]
    assert len(a) == 2
    st, n = a[1]
    return bass.AP(ap.tensor, ap.offset + (n - 1) * st, [a[0], [-st, n]])


NCONST = 1408
C_BD = 1152
C_OFF = 1280
BIG = 30000.0
C_TRIF, C_TRIB, C_MLF, C_MLB, C_GAF, C_GAB, C_GPF, C_GPB = 128, 256, 384, 512, 640, 768, 896, 1024
PJD_ROWS = 4358
PJD_C0 = 2
PJD_L0 = 260
NPJA = 1552
NPJD = 2064


def make_consts():
    c = np.zeros((128, NCONST), np.float32)
    i = np.arange(128)
    s = i[:, None]
    t = i[None, :]
    c[:, 0:128] = np.eye(128, dtype=np.float32)
    c[:, C_TRIF:C_TRIF + 128] = (s <= t)
    c[:, C_TRIB:C_TRIB + 128] = (s >= t)
    c[:, C_MLF:C_MLF + 128] = BIG * (s > t)
    c[:, C_MLB:C_MLB + 128] = BIG * (s < t)
    c[:, C_GAF:C_GAF + 128] = BIG * (t >= s)
    c[:, C_GAB:C_GAB + 128] = BIG * (t <= s)
    c[:, C_GPF:C_GPF + 128] = -BIG * (s > t)
    c[:, C_GPB:C_GPB + 128] = -BIG * (s < t)
    c[:, C_BD:C_BD + 128] = ((s // 64) == (t // 64))
    c[:, C_OFF:C_OFF + 128] = ((s // 64) != (t // 64))
    return c


def declare_mixer_inputs(g, din, dscr):
    P = g.P
    g.ml_gate_b = din("ml_gate_b", [DEPTH, 16])
    g.ml_norm_g = din("ml_norm_g", [DEPTH, 512])
    g.lru_conv_w = din("lru_conv_w", [DEPTH, 4, 512])
    g.lru_conv_b = din("lru_conv_b", [DEPTH, 512])
    g.lru_w_a = din("lru_w_a", [DEPTH, 2, 8, 64, 64])
    g.lru_b_a = din("lru_b_a", [DEPTH, 2, 512])
    g.lru_w_x = din("lru_w_x", [DEPTH, 2, 8, 64, 64])
    g.lru_b_x = din("lru_b_x", [DEPTH, 2, 512])
    g.lru_lambda = din("lru_lambda", [DEPTH, 2, 512])
    g.dn_conv_w = din("dn_conv_w", [DEPTH, 4 * 1536])
    g.dn_a_log = din("dn_a_log", [DEPTH, 8])
    g.dn_dt_bias = din("dn_dt_bias", [DEPTH, 8])
    g.dn_norm_g = din("dn_norm_g", [DEPTH, 128])
    g.w_branch = din("w_branch", [DEPTH, 1536, D])
    g.w_out = din("w_out", [DEPTH, D, D])
    g.PJA = dscr("PJA", [NTOK, NPJA])
    g.PJD = dscr("PJD", [PJD_ROWS, NPJD])
    g.LXT = dscr("LXT", [512, NTOK])
    g.LGT = dscr("LGT", [512, NTOK])
    g.SGT = dscr("SGT", [3072, NTOK], BF16)
    g.YML = dscr("YML", [NTOK, 512])
    g.YDN = dscr("YDN", [NTOK, 512])
    g.YLT = dscr("YLT", [512, NTOK])
    g.QKVN = dscr("QKVN", [NTOK, 1536])
    g.WBR = [dscr(f"WBR{l}", [1536, D], BF16) for l in range(DEPTH)]
    g.WOUT = [dscr(f"WOUT{l}", [D, D], BF16) for l in range(DEPTH)]
    g.WBRb = [P.buf() for l in range(DEPTH)]
    g.WOUTb = [P.buf() for l in range(DEPTH)]


def setup_mixer_weights(g, l):
    P = g.P
    for kc in range(0, 12, 4):
        P.dma('pool', g.WBR[l][kc * 128:(kc + 4) * 128, :], g.w_branch[l, kc * 128:(kc + 4) * 128, :], writes=[g.WBRb[l]])
    for kc in range(0, 8, 4):
        P.dma('pool', g.WOUT[l][kc * 128:(kc + 4) * 128, :], g.w_out[l, kc * 128:(kc + 4) * 128, :], writes=[g.WOUTb[l]])
    if l == 0:
        z = P.sb('zpad', [2, NPJD], F32)
        zb = P.buf()
        P.I('pool', 'memset', z[:], 0.0, writes=[zb])
        for r in (0, 258, 4356):
            P.dma('sp', g.PJD[r:r + 2, :], z[:], reads=[zb])


def pjd_rows(g, tile_idx, col0, ncols):
    if tile_idx < 2:
        r = PJD_C0 + tile_idx * 128
        return [(g.PJD[r:r + 128, col0:col0 + ncols], 0, 128)]
    n0 = (tile_idx - 2) * 128
    r0 = n0 // 64
    res = []
    for h in range(2):
        base = g.PJD[PJD_L0 + r0 + h:PJD_L0 + r0 + h + 1, col0:col0 + ncols]
        a = [list(q) for q in base.ap]
        ap = bass.AP(base.tensor, base.offset, [[64 * NPJD, 64], a[-1]])
        res.append((ap, h * 64, h * 64 + 64))
    return res


def phase_proj(g, l):
    P = g.P
    TM = [(0, 512, 'a'), (512, 512, 'a'), (1024, 512, 'a'), (1536, 16, 'a'),
          (2576, 512, 'd'), (3088, 512, 'd'), (3600, 512, 'd'), (4112, 512, 'd'), (4624, 16, 'd')]
    FM = [(1552, 'x', 0), (2064, 'g', 0)] + [(4640 + 512 * i, 's', i) for i in range(6)]
    with contextlib.ExitStack() as ph:
        mgs = P.sb('mgs', [128, D], F32, ph)
        msh = P.sb('msh', [128, D], F32, ph)
        modb = P.buf()
        xring = Ring(P, 'xt', 2, [128, D], F32, ph)
        hTring = Ring(P, 'hT', 2, [128, 8, 512], BF16, ph)
        wring = Ring(P, 'wp', 3, [128, 8, 512], BF16, ph)
        stg = Ring(P, 'stg', 6, [128, 512], F32, ph)
        stgb = Ring(P, 'stgb', 4, [128, 512], BF16, ph)
        tmp = Ring(P, 'gtmp', 4, [128, 512], F32, ph)
        scr = dict(junk=(P.sb('junk', [128, D], BF16, ph), P.buf()),
                   h1=Ring(P, 'h1', 2, [128, D], F32, ph),
                   hb=Ring(P, 'hb', 2, [128, D], BF16, ph))
        stats = Ring(P, 'st', 16, [128, 1], F32, ph)
        cur_set = None
        ev = 0
        for (row0, nt, s) in BLOCKS:
            NT = nt * 128
            if s != cur_set:
                cur_set = s
                for (tile, m) in ((msh, 3), (mgs, 4)):
                    P.dma('sp', tile[:], g.MODB[s, m], reads=[g.MODBb[s][m]], writes=[modb])
            hT_t, hT_b = hTring.next()
            for t in range(nt):
                x_t, x_b = xring.next()
                prologue_tile(g, row0 + t * 128, x_t, x_b, g.XSb[(row0 + t * 128) // 128], mgs, msh, modb,
                              hT_t, hT_b, t * 128, scr, stats)
            for (c0, ncol, kind) in TM:
                w_t, w_b = wring.next()
                P.dma('sp', w_t[:, :, :ncol], g.WIN[l][:, c0:c0 + ncol].rearrange("(kc p) n -> p kc n", p=128),
                      reads=[g.WINb[l]], writes=[w_b])
                for t in range(nt):
                    pt, pb = g.ps[ev % 6]
                    for kc in range(8):
                        P.mm(pt[:, :ncol], hT_t[:, kc, t * 128:(t + 1) * 128], w_t[:, kc, :ncol], kc == 0, kc == 7,
                             reads=[hT_b, w_b], writes=[pb])
                    st, sb_ = stg.next()
                    if ev % 2 == 0:
                        P.I('act', 'activation', out=st[:, :ncol], in_=pt[:, :ncol], func=AF.Copy, reads=[pb], writes=[sb_])
                    else:
                        P.I('dve', 'tensor_copy', out=st[:, :ncol], in_=pt[:, :ncol], reads=[pb], writes=[sb_])
                    ev += 1
                    ti = (row0 + t * 128) // 128
                    if kind == 'a':
                        P.dma('sp', g.PJA[ti * 128:(ti + 1) * 128, c0:c0 + ncol], st[:, :ncol], reads=[sb_])
                    else:
                        for (dst, plo, phi) in pjd_rows(g, ti, c0 - 2576, ncol):
                            P.dma('sp', dst, st[plo:phi, :ncol], reads=[sb_])
            for (c0, kind, gi) in FM:
                w_t, w_b = wring.next()
                P.dma('sp', w_t[:], g.WIN[l][:, c0:c0 + 512].rearrange("(kc p) n -> p kc n", p=128),
                      reads=[g.WINb[l]], writes=[w_b])
                for fc in range(4):
                    pt, pb = g.ps[ev % 6]
                    ev += 1
                    for kc in range(8):
                        P.mm(pt[:, :NT], w_t[:, kc, fc * 128:(fc + 1) * 128], hT_t[:, kc, :NT], kc == 0, kc == 7,
                             reads=[hT_b, w_b], writes=[pb])
                    if kind == 'x':
                        st, sb_ = stg.next()
                        P.I('act', 'activation', out=st[:, :NT], in_=pt[:, :NT], func=AF.Copy, reads=[pb], writes=[sb_])
                        P.dma('sp', g.LXT[fc * 128:(fc + 1) * 128, row0:row0 + NT], st[:, :NT], reads=[sb_])
                    elif kind == 's':
                        st, sb_ = stgb.next()
                        P.I('act', 'activation', out=st[:, :NT], in_=pt[:, :NT], func=AF.Sigmoid, reads=[pb], writes=[sb_])
                        r = gi * 512 + fc * 128
                        P.dma('sp', g.SGT[r:r + 128, row0:row0 + NT], st[:, :NT], reads=[sb_])
                    else:
                        t1, t1b = tmp.next()
                        t2, t2b = tmp.next()
                        st, sb_ = stg.next()
                        P.I('act', 'activation', out=t1[:, :NT], in_=pt[:, :NT], func=AF.Square, reads=[pb], writes=[t1b])
                        P.I('dve', 'tensor_scalar', out=t1[:, :NT], in0=t1[:, :NT], scalar1=0.044715, scalar2=1.0,
                            op0=ALU.mult, op1=ALU.add, reads=[t1b], writes=[t1b])
                        P.I('dve', 'tensor_tensor', out=t2[:, :NT], in0=t1[:, :NT], in1=pt[:, :NT], op=ALU.mult,
                            reads=[t1b, pb], writes=[t2b])
                        P.I('act', 'activation', out=t2[:, :NT], in_=t2[:, :NT], func=AF.Sigmoid, scale=1.5957691216057308,
                            reads=[t2b], writes=[t2b])
                        P.I('dve', 'tensor_tensor', out=st[:, :NT], in0=t2[:, :NT], in1=pt[:, :NT], op=ALU.mult,
                            reads=[t2b, pb], writes=[sb_])
                        P.dma('sp', g.LGT[fc * 128:(fc + 1) * 128, row0:row0 + NT], st[:, :NT], reads=[sb_])
class K:
    pass


def build(nlayers=DEPTH, stop=None, dbg=False):
    nc = bass.Bass("TRN2", target_bir_lowering=False)
    P = Prog(nc)
    g = K()
    g.nc, g.P = nc, P
    g.dbg = dbg

    def din(name, shape):
        return nc.dram_tensor(name, list(shape), F32, kind="ExternalInput").ap()

    def dscr(name, shape, dt=F32):
        return nc.dram_tensor(name, list(shape), dt, kind="Internal").ap()

    g.x_in = din("x", [LAT, D])
    g.c_in = din("c", [D])
    g.ctx_in = din("ctx", [CTX, D])
    g.cctx_in = din("c_ctx", [D])
    g.w_mod = din("w_mod", [DEPTH, D, 9 * D])
    g.b_mod = din("b_mod", [DEPTH, 9 * D])
    g.norm_g = din("norm_g", [DEPTH, 6 * D])
    g.w_gu = din("ffn_w_gu", [DEPTH, 2, D, 2 * DFF])
    g.w_dn = din("ffn_w_down", [DEPTH, 2, DFF, D])
    g.w_in = din("w_in", [DEPTH, D, DIN])
    g.consts = din("consts", [128, NCONST])
    declare_mixer_inputs(g, din, dscr)
    g.out = nc.dram_tensor("out", [LAT, D], F32, kind="ExternalOutput").ap()

    g.XS = dscr("XS", [NTOK, D])
    g.XSb = [P.buf(f'XS{i}') for i in range(NTILE)]
    g.MODB = dscr("MODB", [2, 9, 128, D])
    g.MODBb = [[P.buf() for _ in range(9)] for _ in range(2)]
    g.WGU = [[dscr(f"WGU{l}_{f}", [D, 2 * DFF], BF16) for f in range(2)] for l in range(DEPTH)]
    g.WD = [[dscr(f"WD{l}_{f}", [DFF, D], BF16) for f in range(2)] for l in range(DEPTH)]
    g.WIN = [dscr(f"WIN{l}", [D, DIN], BF16) for l in range(DEPTH)]
    g.WGUb = [[P.buf() for f in range(2)] for l in range(DEPTH)]
    g.WDb = [[P.buf() for f in range(2)] for l in range(DEPTH)]
    g.WINb = [P.buf() for l in range(DEPTH)]
    g.outb = P.buf('out')
    g.dbgb = P.buf('dbg')
    g.dbg_t = {}
    if dbg:
        for (nm, shp) in dbg:
            g.dbg_t[nm] = nc.dram_tensor("dbg_" + nm, list(shp), F32, kind="ExternalOutput").ap()

    g.ps = [(P.ps(f'ps{i}', [128, 512], F32), P.buf(f'ps{i}', excl=True)) for i in range(7)]
    g.psT = (P.ps('psT', [128, 1024], BF16), P.buf('psT', excl=True))

    g.cst = P.sb('cst', [128, NCONST], F32)
    g.cstb = P.buf('cst')
    P.dma('sp', g.cst[:], g.consts[:, :], writes=[g.cstb])
    g.ident = g.cst[:, 0:128]
    g.identb = P.sb('identb', [128, 128], BF16)
    g.identbb = P.buf()
    P.I('dve', 'tensor_copy', out=g.identb[:], in_=g.ident, reads=[g.cstb], writes=[g.identbb])
    g.negh = P.sb('negh', [128, 1], F32)
    g.neghb = P.buf()
    P.I('pool', 'memset', g.negh[:], -0.5, writes=[g.neghb])

    setup(g)
    for l in range(nlayers):
        last = l == DEPTH - 1
        phase_mod(g, l)
        P.barrier()
        phase_ffn(g, l, 0, False)
        P.barrier()
        if stop == 'ffn1':
            break
        phase_proj(g, l)
        P.barrier()
        if stop == 'proj':
            break
        phase_mixers(g, l, stop)
        P.barrier()
        if stop in ('ml', 'lru', 'dn'):
            break
        phase_merge(g, l, last)
        P.barrier()
        if stop == 'merge':
            break
        phase_ffn(g, l, 1, last)
        P.barrier()
    if 'x' in g.dbg_t:
        for i in range(NTILE):
            P.dma('sp', g.dbg_t['x'][i * 128:(i + 1) * 128, :], g.XS[i * 128:(i + 1) * 128, :],
                  reads=[g.XSb[i]], writes=[g.dbgb])
    if 'mod' in g.dbg_t:
        for s in range(2):
            for m in range(9):
                P.dma('sp', g.dbg_t['mod'][s, m], g.MODB[s, m], reads=[g.MODBb[s][m]], writes=[g.dbgb])
    for nm, ap_ in g.dbg_t.items():
        if nm in ('x', 'mod'):
            continue
        src = getattr(g, nm)
        nrow = ap_.shape[0]
        for r in range(0, nrow, 512):
            r1 = min(nrow, r + 512)
            P.dma('sp', ap_[r:r1, :], src[r:r1, :], writes=[g.dbgb])
    P.wait('sp', [g.dbgb, g.outb])
    P.barrier()
    P.emit()
    P.close()
    return nc


def setup(g):
    P = g.P
    for i in range(NTILE):
        if i < 2:
            src = g.ctx_in[i * 128:(i + 1) * 128, :]
        else:
            src = g.x_in[(i - 2) * 128:(i - 1) * 128, :]
        P.dma('sp', g.XS[i * 128:(i + 1) * 128, :], src, writes=[g.XSb[i]])
    for l in range(DEPTH):
        for f in range(2):
            for kc in range(8):
                P.dma('pool', g.WGU[l][f][kc * 128:(kc + 1) * 128, :], g.w_gu[l, f, kc * 128:(kc + 1) * 128, :],
                      writes=[g.WGUb[l][f]])
            for kc in range(0, 22, 2):
                P.dma('pool', g.WD[l][f][kc * 128:(kc + 2) * 128, :], g.w_dn[l, f, kc * 128:(kc + 2) * 128, :],
                      writes=[g.WDb[l][f]])
            if f == 0:
                for kc in range(8):
                    P.dma('pool', g.WIN[l][kc * 128:(kc + 1) * 128, :], g.w_in[l, kc * 128:(kc + 1) * 128, :],
                          writes=[g.WINb[l]])
                setup_mixer_weights(g, l)


def phase_mod(g, l):
    P = g.P
    with contextlib.ExitStack() as ph:
        craw = P.sb('craw', [128, 2, 8], F32, ph)
        sc = P.sb('sc', [128, 2, 8], F32, ph)
        crb, scb = P.buf(), P.buf()
        P.dma('sp', craw[:, 0, :], g.c_in.rearrange("(kc p) -> p kc", p=128), writes=[crb], allow_slow_non_contiguous=True)
        P.dma('sp', craw[:, 1, :], g.cctx_in.rearrange("(kc p) -> p kc", p=128), writes=[crb], allow_slow_non_contiguous=True)
        P.I('act', 'activation', out=sc[:], in_=craw[:], func=AF.Silu, reads=[crb], writes=[scb])
        ngb = P.sb('ngb', [128, 6 * D], F32, ph)
        ngbb = P.buf()
        P.dma('sp', ngb[:], bcast_part(g.norm_g[l]), writes=[ngbb])
        wring = Ring(P, 'wm', 2, [128, 8, 512], F32, ph)
        bring = Ring(P, 'bm', 2, [128, 512], F32, ph)
        mring = Ring(P, 'mv', 4, [128, 512], F32, ph)
        for blk in range(18):
            wt, wb = wring.next()
            bt, bb = bring.next()
            c0 = blk * 512
            for h2 in range(2):
                P.dma('sp', wt[:, h2 * 4:(h2 + 1) * 4, :],
                      g.w_mod[l, h2 * 512:(h2 + 1) * 512, c0:c0 + 512].rearrange("(kc p) n -> p kc n", p=128), writes=[wb])
            P.dma('sp', bt[:], bcast_part(g.b_mod[l, c0:c0 + 512]), writes=[bb])
            m = blk // 2
            half = blk % 2
            kind = m % 3
            j = m // 3
            for s in range(2):
                pt, pb = g.ps[(blk * 2 + s) % 4]
                for kc in range(8):
                    P.mm(pt[:], bcast_free(sc[:, s, kc:kc + 1], 128), wt[:, kc, :], kc == 0, kc == 7,
                         reads=[scb, wb], writes=[pb])
                mt, mb = mring.next()
                P.I('dve', 'tensor_tensor', out=mt[:], in0=pt[:], in1=bt[:], op=ALU.add, reads=[pb, bb], writes=[mb])
                if kind == 1:
                    o = (2 * j) * D + half * 512
                    P.I('dve', 'scalar_tensor_tensor', out=mt[:], in0=mt[:], scalar=1.0, in1=ngb[:, o:o + 512],
                        op0=ALU.add, op1=ALU.mult, reads=[mb, ngbb], writes=[mb])
                elif kind == 2:
                    fac = 1.0 if j == 1 else 0.5
                    o = (2 * j + 1) * D + half * 512
                    P.I('dve', 'scalar_tensor_tensor', out=mt[:], in0=mt[:], scalar=fac, in1=ngb[:, o:o + 512],
                        op0=ALU.mult, op1=ALU.mult, reads=[mb, ngbb], writes=[mb])
                P.dma('sp', g.MODB[s, m, :, half * 512:(half + 1) * 512], mt[:], reads=[mb], writes=[g.MODBb[s][m]])


def rms_rstd(g, src_ap, src_bufs, junk, junkb, stats, n=D):
    P = g.P
    (ss, ssb) = stats.next()
    (rs, rsb) = stats.next()
    P.I('act', 'activation', out=junk, in_=src_ap, func=AF.Square, accum_out=ss[:], reads=list(src_bufs), writes=[junkb, ssb])
    P.I('dve', 'tensor_scalar', out=ss[:], in0=ss[:], scalar1=1.0 / n, scalar2=EPS, op0=ALU.mult, op1=ALU.add,
        reads=[ssb], writes=[ssb])
    P.I('pool', 'tensor_tensor', out=rs[:], in0=ss[:], in1=g.negh[:], op=ALU.pow, reads=[ssb, g.neghb], writes=[rsb])
    return rs, rsb


def prologue_tile(g, row0, x_t, x_b, xsbuf, modgs, modsh, modb, hT_t, hT_b, tcol, scr, stats):
    P = g.P
    P.dma('sp', x_t[:], g.XS[row0:row0 + 128, :], reads=[xsbuf], writes=[x_b])
    junk, junkb = scr['junk']
    rs, rsb = rms_rstd(g, x_t[:], [x_b], junk[:], junkb, stats)
    h1, h1b = scr['h1'].next()
    hb, hbb = scr['hb'].next()
    P.I('dve', 'scalar_tensor_tensor', out=h1[:], in0=x_t[:], scalar=rs[:], in1=modgs[:], op0=ALU.mult, op1=ALU.mult,
        reads=[x_b, rsb, modb], writes=[h1b])
    P.I('pool', 'tensor_tensor', out=hb[:], in0=h1[:], in1=modsh[:], op=ALU.add, reads=[h1b, modb], writes=[hbb])
    pT, pTb = g.psT
    for kc in range(8):
        P.I('pe', 'transpose', pT[:, kc * 128:(kc + 1) * 128], hb[:, kc * 128:(kc + 1) * 128], g.identb[:],
            reads=[hbb, g.identbb], writes=[pTb])
    P.I('act', 'activation', out=hT_t[:, :, tcol:tcol + 128], in_=pT[:].rearrange("p (k t) -> p k t", k=8), func=AF.Copy,
        reads=[pTb], writes=[hT_b])


BLOCKS = [(0, 2, 1)] + [(256 + 512 * i, 4, 0) for i in range(8)]


def phase_ffn(g, l, f, last):
    P = g.P
    j = 0 if f == 0 else 2
    with contextlib.ExitStack() as ph:
        wd = P.sb('wd', [128, 22, D], BF16, ph)
        wdb = P.buf()
        for h2 in range(2):
            P.dma('sp', wd[:, h2 * 11:(h2 + 1) * 11, :],
                  g.WD[l][f][h2 * 1408:(h2 + 1) * 1408, :].rearrange("(kc p) n -> p kc n", p=128),
                  reads=[g.WDb[l][f]], writes=[wdb])
        mgs = P.sb('mgs', [128, D], F32, ph)
        msh = P.sb('msh', [128, D], F32, ph)
        mgg = P.sb('mgg', [128, D], F32, ph)
        modb = P.buf()
        guring = Ring(P, 'gu', 3, [128, 2, 8, 256], BF16, ph)
        xring = Ring(P, 'xt', 8, [128, D], F32, ph)
        hTring = Ring(P, 'hT', 2, [128, 8, 512], BF16, ph)
        aT = P.sb('aT', [128, 22, 512], BF16, ph)
        aTb = P.buf()
        scr = dict(junk=(P.sb('junk', [128, D], BF16, ph), P.buf()),
                   h1=Ring(P, 'h1', 2, [128, D], F32, ph),
                   hb=Ring(P, 'hb', 2, [128, D], BF16, ph))
        sgring = Ring(P, 'sg', 2, [128, 512], F32, ph)
        yring = Ring(P, 'ysb', 2, [128, D], F32, ph)
        stats = Ring(P, 'st', 16, [128, 1], F32, ph)
        blocks = BLOCKS[1:] if last else BLOCKS
        cur_set = None
        for (row0, nt, s) in blocks:
            NT = nt * 128
            if s != cur_set:
                cur_set = s
                for (tile, m) in ((msh, 3 * j), (mgs, 3 * j + 1), (mgg, 3 * j + 2)):
                    P.dma('sp', tile[:], g.MODB[s, m], reads=[g.MODBb[s][m]], writes=[modb])
            hT_t, hT_b = hTring.next()
            xs = []
            for t in range(nt):
                x_t, x_b = xring.next()
                xs.append((x_t, x_b))
                prologue_tile(g, row0 + t * 128, x_t, x_b, g.XSb[(row0 + t * 128) // 128], mgs, msh, modb,
                              hT_t, hT_b, t * 128, scr, stats)
            for g2 in range(11):
                w_t, w_b = guring.next()
                for gu in range(2):
                    c0 = gu * DFF + g2 * 256
                    P.dma('sp', w_t[:, gu, :, :], g.WGU[l][f][:, c0:c0 + 256].rearrange("(kc p) n -> p kc n", p=128),
                          reads=[g.WGUb[l][f]], writes=[w_b])
                for ff in range(2):
                    ffc = g2 * 2 + ff
                    pg, pgb = g.ps[ffc % 2]
                    pu, pub = g.ps[2 + ffc % 2]
                    for (pt, pb, gu) in ((pg, pgb, 0), (pu, pub, 1)):
                        for kc in range(8):
                            P.mm(pt[:, :NT], w_t[:, gu, kc, ff * 128:(ff + 1) * 128], hT_t[:, kc, :NT], kc == 0, kc == 7,
                                 reads=[w_b, hT_b], writes=[pb])
                    sg, sgb = sgring.next()
                    P.I('act', 'activation', out=sg[:, :NT], in_=pg[:, :NT], func=AF.Silu, reads=[pgb], writes=[sgb])
                    P.I('dve', 'tensor_tensor', out=aT[:, ffc, :NT], in0=pu[:, :NT], in1=sg[:, :NT], op=ALU.mult,
                        reads=[pub, sgb], writes=[aTb])
            for t in range(nt):
                x_t, x_b = xs[t]
                ysb, ysbb = yring.next()
                for cb in range(2):
                    py, pyb = g.ps[4 + cb]
                    for kc in range(22):
                        P.mm(py[:], aT[:, kc, t * 128:(t + 1) * 128], wd[:, kc, cb * 512:(cb + 1) * 512], kc == 0, kc == 21,
                             reads=[aTb, wdb], writes=[pyb])
                    P.I('act', 'activation', out=ysb[:, cb * 512:(cb + 1) * 512], in_=py[:], func=AF.Copy,
                        reads=[pyb], writes=[ysbb])
                junk, junkb = scr['junk']
                rs, rsb = rms_rstd(g, ysb[:], [ysbb], junk[:], junkb, stats)
                P.I('dve', 'scalar_tensor_tensor', out=ysb[:], in0=ysb[:], scalar=rs[:], in1=mgg[:], op0=ALU.mult, op1=ALU.mult,
                    reads=[ysbb, rsb, modb], writes=[ysbb])
                P.I('pool', 'tensor_tensor', out=x_t[:], in0=x_t[:], in1=ysb[:], op=ALU.add, reads=[ysbb, x_b], writes=[x_b])
                r0 = row0 + t * 128
                if last:
                    P.dma('sp', g.out[r0 - 256:r0 - 128, :], x_t[:], reads=[x_b], writes=[g.outb])
                else:
                    P.dma('sp', g.XS[r0:r0 + 128, :], x_t[:], reads=[x_b], writes=[g.XSb[r0 // 128]])
def phase_mixers(g, l, stop):
    P = g.P
    phase_lru(g, l)
    P.barrier()
    if stop == 'lru':
        return
    phase_ml(g, l)
    P.barrier()
    if stop == 'ml':
        return
    phase_dn(g, l)


SEQ_BLOCKS = [(0, 256)] + [(256 + 512 * i, 512) for i in range(8)]


def phase_lru(g, l):
    P = g.P
    NP = NTOK + 6
    with contextlib.ExitStack() as ph:
        cw = P.sb('cw', [128, 4, 4], F32, ph)
        cb = P.sb('cb', [128, 4], F32, ph)
        bab = P.sb('bab', [128, 2, 4], F32, ph)
        bxb = P.sb('bxb', [128, 2, 4], F32, ph)
        lam = P.sb('lam', [128, 2, 4], F32, ph)
        c1 = P.sb('c1', [128, 2, 4], F32, ph)
        prm = P.buf()
        for j in range(4):
            P.dma('sp', cw[:, :, j], g.lru_conv_w[l, j].rearrange("(cg p) -> p cg", p=128), writes=[prm], allow_slow_non_contiguous=True)
        P.dma('sp', cb[:], g.lru_conv_b[l].rearrange("(cg p) -> p cg", p=128), writes=[prm], allow_slow_non_contiguous=True)
        for (t_, src) in ((bab, g.lru_b_a), (bxb, g.lru_b_x), (lam, g.lru_lambda)):
            for d in range(2):
                P.dma('sp', t_[:, d, :], src[l, d].rearrange("(cg p) -> p cg", p=128), writes=[prm], allow_slow_non_contiguous=True)
        c1b = P.buf()
        P.I('act', 'activation', out=c1[:], in_=lam[:], func=AF.Exp, scale=-1.0, reads=[prm], writes=[c1b])
        P.I('act', 'activation', out=c1[:], in_=c1[:], func=AF.Ln, bias=1.0, reads=[c1b], writes=[c1b])
        P.I('dve', 'tensor_scalar', out=c1[:], in0=c1[:], scalar1=-8.0, scalar2=None, op0=ALU.mult, reads=[c1b], writes=[c1b])
        wbd = P.sb('wbd', [128, 2, 2, 4, 128], F32, ph)
        wbdb = P.buf()
        P.I('pool', 'memset', wbd[:], 0.0, writes=[wbdb])
        for d in range(2):
            for ax, src in enumerate((g.lru_w_a, g.lru_w_x)):
                for half in range(2):
                    s_ap = src[l, d].rearrange("(cg h) i j -> h i cg j", h=2)[half]
                    P.dma('sp', wbd[half * 64:(half + 1) * 64, d, ax, :, half * 64:(half + 1) * 64], s_ap, writes=[wbdb])
        xpad = P.sb('xpad', [128, NP], F32, ph)
        xc = P.sb('xc', [128, NTOK], F32, ph)
        av = [P.sb(f'a{d}', [128, NTOK], F32, ph) for d in range(2)]
        bxv = [P.sb(f'bx{d}', [128, NTOK], F32, ph) for d in range(2)]
        hv = [P.sb(f'h{d}', [128, NTOK], F32, ph) for d in range(2)]
        gel = P.sb('gel', [128, NTOK], F32, ph)
        xpb, xcb, gelb = P.buf(), P.buf(), P.buf()
        ab = [P.buf(), P.buf()]
        bxb_ = [P.buf(), P.buf()]
        hb = [P.buf(), P.buf()]
        tmp, tmpb = hv[1], hb[1]
        for (lo, hi) in ((0, 2), (258, 261), (NP - 1, NP)):
            P.I('pool', 'memset', xpad[:, lo:hi], 0.0, writes=[xpb])
        for cg in range(4):
            rows = slice(cg * 128, (cg + 1) * 128)
            P.dma('sp', xpad[:, 2:258], g.LXT[rows, 0:256], writes=[xpb])
            P.dma('sp', xpad[:, 261:261 + LAT], g.LXT[rows, 256:NTOK], writes=[xpb])
            P.dma('sp', gel[:], g.LGT[rows, :], writes=[gelb])
            for (o0, n, p0) in ((0, 256, 0), (256, LAT, 259)):
                P.I('dve', 'tensor_scalar', out=xc[:, o0:o0 + n], in0=xpad[:, p0:p0 + n], scalar1=cw[:, cg, 0:1],
                    scalar2=cb[:, cg:cg + 1], op0=ALU.mult, op1=ALU.add, reads=[xpb, prm], writes=[xcb])
                for j in range(1, 4):
                    P.I('dve', 'scalar_tensor_tensor', out=xc[:, o0:o0 + n], in0=xpad[:, p0 + j:p0 + j + n],
                        scalar=cw[:, cg, j:j + 1], in1=xc[:, o0:o0 + n], op0=ALU.mult, op1=ALU.add,
                        reads=[xpb, prm, xcb], writes=[xcb])
            k = 0
            for d in range(2):
                for (c0, n) in SEQ_BLOCKS:
                    for ax, (dst, dstb, bias) in enumerate(((av[d], ab[d], bab), (bxv[d], bxb_[d], bxb))):
                        pt, pb = g.ps[k % 6]
                        k += 1
                        P.mm(pt[:, :n], wbd[:, d, ax, cg, :], xc[:, c0:c0 + n], True, True, reads=[wbdb, xcb], writes=[pb])
                        P.I('act', 'activation', out=dst[:, c0:c0 + n], in_=pt[:, :n], func=AF.Sigmoid,
                            bias=bias[:, d, cg:cg + 1], reads=[pb, prm], writes=[dstb])
            for d in range(2):
                P.I('act', 'activation', out=av[d][:], in_=av[d][:], func=AF.Exp, scale=c1[:, d, cg:cg + 1],
                    reads=[ab[d], c1b], writes=[ab[d]])
            for d in range(2):
                P.I('dve', 'tensor_tensor', out=tmp[:], in0=av[d][:], in1=av[d][:], op=ALU.mult, reads=[ab[d]], writes=[tmpb])
                P.I('dve', 'tensor_scalar', out=tmp[:], in0=tmp[:], scalar1=-1.0, scalar2=1.0, op0=ALU.mult, op1=ALU.add,
                    reads=[tmpb], writes=[tmpb])
                P.I('act', 'activation', out=tmp[:], in_=tmp[:], func=AF.Sqrt, reads=[tmpb], writes=[tmpb])
                P.I('pool', 'tensor_tensor', out=bxv[d][:], in0=bxv[d][:], in1=xc[:], op=ALU.mult, reads=[bxb_[d], xcb], writes=[bxb_[d]])
                P.I('dve', 'tensor_tensor', out=bxv[d][:], in0=bxv[d][:], in1=tmp[:], op=ALU.mult, reads=[bxb_[d], tmpb], writes=[bxb_[d]])
            P.I('dve', 'tensor_tensor_scan', out=hv[0][:, 0:256], data0=av[0][:, 0:256], data1=bxv[0][:, 0:256], initial=0.0,
                op0=ALU.mult, op1=ALU.add, reads=[ab[0], bxb_[0]], writes=[hb[0]])
            P.I('dve', 'tensor_tensor_scan', out=hv[0][:, 256:NTOK], data0=av[0][:, 256:NTOK], data1=bxv[0][:, 256:NTOK],
                initial=hv[0][:, 255:256], op0=ALU.mult, op1=ALU.add, reads=[ab[0], bxb_[0], hb[0]], writes=[hb[0]])
            P.I('dve', 'tensor_tensor_scan', out=rev_free(hv[1][:, 0:256]), data0=rev_free(av[1][:, 0:256]),
                data1=rev_free(bxv[1][:, 0:256]), initial=0.0, op0=ALU.mult, op1=ALU.add,
                reads=[ab[1], bxb_[1]], writes=[hb[1]])
            P.I('dve', 'tensor_tensor_scan', out=rev_free(hv[1][:, 256:NTOK]), data0=rev_free(av[1][:, 256:NTOK]),
                data1=rev_free(bxv[1][:, 256:NTOK]), initial=hv[1][:, 0:1], op0=ALU.mult, op1=ALU.add,
                reads=[ab[1], bxb_[1], hb[1]], writes=[hb[1]])
            P.I('pool', 'tensor_tensor', out=hv[0][:], in0=hv[0][:], in1=hv[1][:], op=ALU.add, reads=[hb[0], hb[1]], writes=[hb[0]])
            P.I('dve', 'tensor_tensor', out=hv[0][:], in0=hv[0][:], in1=gel[:], op=ALU.mult, reads=[hb[0], gelb], writes=[hb[0]])
            P.dma('sp', g.YLT[rows, :], hv[0][:], reads=[hb[0]])


DEBUG_CTX_ONLY = False
DN_STOP_AT = 0
KQ_ENG = 'dve'
DN_SUB = 0


def chunk_order(d):
    if d == 0:
        return list(range(NTILE))
    return [1, 0] + list(range(NTILE - 1, 1, -1))


def phase_ml(g, l):
    P = g.P
    with contextlib.ExitStack() as ph:
        gbias = P.sb('gbias', [128, 16], F32, ph)
        mlg = P.sb('mlg', [128, 512], F32, ph)
        prm = P.buf()
        P.dma('sp', gbias[:], bcast_part(g.ml_gate_b[l]), writes=[prm])
        P.dma('sp', mlg[:], bcast_part(g.ml_norm_g[l]), writes=[prm])
        HF = P.sb('HF', [128, NTILE, 512], F32, ph)
        HFb = [P.buf() for _ in range(NTILE)]
        Caug = P.sb('Caug', [128, 2, 2, 129], F32, ph)
        Cb = [[P.buf() for _ in range(4)] for _ in range(2)]
        cinit = P.buf()
        P.I('pool', 'memset', Caug[:], 0.0, writes=[cinit] + Cb[0] + Cb[1])
        pjar = Ring(P, 'pja', 2, [128, NPJA], F32, ph)
        vaugr = Ring(P, 'vaug', 2, [128, 4, 129], F32, ph)
        for (t_, b_) in vaugr.slots:
            P.I('pool', 'memset', t_[:, :, 128:129], 1.0, writes=[b_])
        sm = Ring(P, 'sm', 12, [128, 16], F32, ph)
        qsr = Ring(P, 'qs', 2, [128, 256], F32, ph)
        TTr = Ring(P, 'TT', 2, [128, 6, 128], F32, ph)
        DTr = Ring(P, 'DT', 3, [128, 128], F32, ph)
        PTr = Ring(P, 'PT', 3, [128, 128], F32, ph)
        kwr = [Ring(P, f'kw{hh}', 2, [128, 128], F32, ph) for hh in range(2)]
        for hh in range(2):
            for (t_, b_) in kwr[hh].slots:
                P.I('pool', 'memset', t_[:], 0.0, writes=[b_])
        elr = Ring(P, 'el', 4, [128, 1], F32, ph)
        st1 = Ring(P, 'st1', 8, [128, 1], F32, ph)
        hsr = Ring(P, 'hs', 2, [128, 512], F32, ph)
        sqr = Ring(P, 'sq', 2, [128, 512], F32, ph)
        sor = Ring(P, 'so', 2, [128, 512], F32, ph)
        pk = 0
        for d in range(2):
            TRI = g.cst[:, (C_TRIF if d == 0 else C_TRIB):(C_TRIF if d == 0 else C_TRIB) + 128]
            MASK = g.cst[:, (C_MLF if d == 0 else C_MLB):(C_MLF if d == 0 else C_MLB) + 128]
            tl = 127 if d == 0 else 0
            for c in chunk_order(d):
                pja, pjab = pjar.next()
                P.dma('sp', pja[:], g.PJA[c * 128:(c + 1) * 128, :], writes=[pjab])
                gb, gbb = sm.next()
                lfn, lfnb = sm.next()
                nb, nbb = sm.next()
                bmi, bmib = sm.next()
                eb, ebb = sm.next()
                P.I('dve', 'tensor_tensor', out=gb[:], in0=pja[:, 1536:1552], in1=gbias[:], op=ALU.add, reads=[pjab, prm], writes=[gbb])
                P.I('act', 'activation', out=lfn[:, 0:4], in_=gb[:, 8 + 4 * d:12 + 4 * d], func=AF.Exp, scale=-1.0, reads=[gbb], writes=[lfnb])
                P.I('act', 'activation', out=lfn[:, 0:4], in_=lfn[:, 0:4], func=AF.Ln, bias=1.0, reads=[lfnb], writes=[lfnb])
                pt, pb = g.ps[pk % 7]
                pk += 1
                P.mm(pt[:, 0:4], TRI, lfn[:, 0:4], True, True, reads=[g.cstb, lfnb], writes=[pb])
                P.I('dve', 'tensor_copy', out=nb[:, 0:4], in_=pt[:, 0:4], reads=[pb], writes=[nbb])
                P.I('dve', 'tensor_tensor', out=bmi[:, 0:4], in0=gb[:, 4 * d:4 * d + 4], in1=nb[:, 0:4], op=ALU.add, reads=[gbb, nbb], writes=[bmib])
                P.I('act', 'activation', out=eb[:, 0:4], in_=nb[:, 0:4], func=AF.Exp, scale=-1.0, reads=[nbb], writes=[ebb])
                qs, qsb = qsr.next()
                eb_bc = bc3(eb[:, 0:4], 64)
                P.I('dve', 'scalar_tensor_tensor', out=qs[:].rearrange("p (h e) -> p h e", h=4),
                    in0=pja[:, 0:256].rearrange("p (h e) -> p h e", h=4), scalar=0.125, in1=eb_bc,
                    op0=ALU.mult, op1=ALU.mult, reads=[pjab, ebb], writes=[qsb])
                TT, TTb = TTr.next()
                ptA, pbA = g.ps[pk % 7]
                pk += 1
                ptB, pbB = g.ps[pk % 7]
                pk += 1
                srcs = [pja[:, 0:128], pja[:, 128:256], pja[:, 256:384], pja[:, 384:512]]
                for i, s_ in enumerate(srcs):
                    P.I('pe', 'transpose', ptA[:, i * 128:(i + 1) * 128], s_, g.ident, reads=[pjab, g.cstb], writes=[pbA])
                for i in range(2):
                    P.I('pe', 'transpose', ptB[:, i * 128:(i + 1) * 128], qs[:, i * 128:(i + 1) * 128], g.ident,
                        reads=[qsb, g.cstb], writes=[pbB])
                P.I('act', 'activation', out=TT[:, 0:4, :], in_=ptA[:, 0:512].rearrange("p (a t) -> p a t", a=4), func=AF.Copy,
                    reads=[pbA], writes=[TTb])
                P.I('dve', 'tensor_copy', out=TT[:, 4:6, :], in_=ptB[:, 0:256].rearrange("p (a t) -> p a t", a=2),
                    reads=[pbB], writes=[TTb])
                vaug, vaugb = vaugr.next()
                P.I('pool', 'tensor_copy', out=vaug[:, :, 0:128], in_=pja[:, 512:1024].rearrange("p (h e) -> p h e", h=4),
                    reads=[pjab], writes=[vaugb])
                if d == 1:
                    hs, hsb = hsr.next()
                for h in range(4):
                    pair, hh = h // 2, h % 2
                    rows = slice(hh * 64, hh * 64 + 64)
                    qT = TT[rows, pair, :]
                    kT = TT[rows, 2 + pair, :]
                    qsT = TT[rows, 4 + pair, :]
                    pbm, pbmb = g.ps[pk % 7]
                    pk += 1
                    P.mm(pbm[:, 0:128], bcast_free(lfn[:, h:h + 1], 128), TRI, True, False, reads=[lfnb, g.cstb], writes=[pbmb])
                    P.mm(pbm[:, 0:128], g.ident, MASK, False, True, reads=[g.cstb], writes=[pbmb])
                    DT, DTb = DTr.next()
                    el, elb = elr.next()
                    P.I('act', 'activation', out=DT[:], in_=pbm[:, 0:128], func=AF.Exp, scale=-1.0, bias=bmi[:, h:h + 1],
                        reads=[pbmb, bmib], writes=[DTb])
                    P.I('act', 'activation', out=el[:], in_=pbm[:, tl:tl + 1], func=AF.Exp, scale=-1.0, reads=[pbmb], writes=[elb])
                    pst, pstb = g.ps[pk % 7]
                    pk += 1
                    P.mm(pst[:, 0:128], kT, qT, True, True, reads=[TTb], writes=[pstb])
                    PT, PTb = PTr.next()
                    P.I('dve', 'scalar_tensor_tensor', out=PT[:], in0=pst[:, 0:128], scalar=0.125, in1=DT[:], op0=ALU.mult, op1=ALU.mult,
                        reads=[pstb, DTb], writes=[PTb])
                    ph_, phb = g.ps[pk % 7]
                    pk += 1
                    P.mm(ph_[:, 0:129], PT[:], vaug[:, h, :], True, False, reads=[PTb, vaugb], writes=[phb])
                    P.mm(ph_[:, 0:129], qsT, Caug[rows, d, pair, :], False, True, reads=[TTb, Cb[d][h]], writes=[phb])
                    kw, kwb = kwr[hh].next()
                    P.I('pool', 'tensor_scalar', out=kw[:, hh * 64:hh * 64 + 64], in0=pja[:, 256 + h * 64:256 + h * 64 + 64],
                        scalar1=DT[:, tl:tl + 1], scalar2=None, op0=ALU.mult, reads=[pjab, DTb], writes=[kwb])
                    pu, pub = g.ps[pk % 7]
                    pk += 1
                    P.mm(pu[:, 0:129], kw[:], vaug[:, h, :], True, True, reads=[kwb, vaugb], writes=[pub])
                    P.I('dve', 'scalar_tensor_tensor', out=Caug[rows, d, pair, :], in0=Caug[rows, d, pair, :], scalar=el[rows, :],
                        in1=pu[rows, 0:129], op0=ALU.mult, op1=ALU.add, reads=[Cb[d][h], elb, pub], writes=[Cb[d][h]])
                    den, denb = st1.next()
                    P.I('act', 'activation', out=den[:], in_=ph_[:, 128:129], func=AF.Abs, reads=[phb], writes=[denb])
                    P.I('dve', 'tensor_scalar', out=den[:], in0=den[:], scalar1=1.0, scalar2=None, op0=ALU.max, reads=[denb], writes=[denb])
                    P.I('dve', 'reciprocal', out=den[:], in_=den[:], reads=[denb], writes=[denb])
                    if d == 0:
                        P.I('act', 'activation', out=HF[:, c, h * 128:(h + 1) * 128], in_=ph_[:, 0:128], func=AF.Copy, scale=den[:],
                            reads=[phb, denb], writes=[HFb[c]])
                    else:
                        P.I('dve', 'scalar_tensor_tensor', out=hs[:, h * 128:(h + 1) * 128], in0=ph_[:, 0:128], scalar=den[:],
                            in1=HF[:, c, h * 128:(h + 1) * 128], op0=ALU.mult, op1=ALU.add, reads=[phb, denb, HFb[c]], writes=[hsb])
                if d == 1:
                    finalize_heads(g, hs, hsb, mlg[:], prm, pja[:, 1024:1536], pjab, AF.Sigmoid, sqr, sor, sm,
                                   [(g.YML[c * 128:(c + 1) * 128, :], 0, 128)])


def finalize_heads(g, hs, hsb, gn_ap, gnb, gate_ap, gateb, gate_func, sqr, sor, sm, dsts, gn3=False):
    P = g.P
    sq, sqb = sqr.next()
    so, sob = sor.next()
    ss, ssb = sm.next()
    P.I('pool', 'tensor_tensor', out=sq[:], in0=hs[:], in1=hs[:], op=ALU.mult, reads=[hsb], writes=[sqb])
    P.I('dve', 'tensor_reduce', out=ss[:, 0:4], in_=sq[:].rearrange("p (h e) -> p h e", h=4), axis=mybir.AxisListType.X, op=ALU.add,
        reads=[sqb], writes=[ssb])
    P.I('dve', 'tensor_scalar', out=ss[:, 0:4], in0=ss[:, 0:4], scalar1=1.0 / 128, scalar2=EPS, op0=ALU.mult, op1=ALU.add,
        reads=[ssb], writes=[ssb])
    P.I('pool', 'tensor_tensor', out=ss[:, 4:8], in0=ss[:, 0:4], in1=bcast_free(g.negh[:], 4), op=ALU.pow, reads=[ssb, g.neghb], writes=[ssb])
    rs = ss[:, 4:8]
    rs_bc = bass.AP(rs.tensor, rs.offset, [list(rs.ap[0]), [1, 4], [0, 128]])
    P.I('dve', 'tensor_tensor', out=sq[:].rearrange("p (h e) -> p h e", h=4), in0=hs[:].rearrange("p (h e) -> p h e", h=4),
        in1=rs_bc, op=ALU.mult, reads=[hsb, ssb], writes=[sqb])
    if gn3:
        a = [list(q) for q in gn_ap.ap]
        gn_bc = bass.AP(gn_ap.tensor, gn_ap.offset, [a[0], [0, 4], a[1]])
        P.I('dve', 'tensor_tensor', out=sq[:].rearrange("p (h e) -> p h e", h=4), in0=sq[:].rearrange("p (h e) -> p h e", h=4),
            in1=gn_bc, op=ALU.mult, reads=[sqb, gnb], writes=[sqb])
    else:
        P.I('pool', 'tensor_tensor', out=sq[:], in0=sq[:], in1=gn_ap, op=ALU.mult, reads=[sqb, gnb], writes=[sqb])
    P.I('act', 'activation', out=so[:], in_=gate_ap, func=gate_func, reads=[gateb], writes=[sob])
    P.I('dve', 'tensor_tensor', out=sq[:], in0=sq[:], in1=so[:], op=ALU.mult, reads=[sqb, sob], writes=[sqb])
    for (dst, plo, phi) in dsts:
        P.dma('sp', dst, sq[plo:phi, :], reads=[sqb])
def bc3(ap2, m):
    a = [list(q) for q in ap2.ap]
    return bass.AP(ap2.tensor, ap2.offset, [a[0], a[1], [0, m]])


def dn_rows(c):
    return PJD_C0 + c * 128 if c < 2 else PJD_L0 + (c - 2) * 128


def ydn_dsts(g, c):
    if c < 2:
        return [(g.YDN[c * 128:(c + 1) * 128, :], 0, 128)]
    col0 = 2 * (c - 2)
    res = []
    for h in range(2):
        base = g.YDN[256 + col0 + h:256 + col0 + h + 1, :]
        a = [list(q) for q in base.ap]
        res.append((bass.AP(base.tensor, base.offset, [[64 * 512, 64], a[-1]]), h * 64, h * 64 + 64))
    return res


def phase_dn(g, l):
    P = g.P
    with contextlib.ExitStack() as ph:
        cwb = P.sb('cwb', [128, 4 * 1536], F32, ph)
        dng = P.sb('dng', [128, 128], F32, ph)
        dtb = P.sb('dtb', [128, 8], F32, ph)
        nea = P.sb('nea', [128, 8], F32, ph)
        prm = P.buf()
        P.dma('sp', cwb[:], bcast_part(g.dn_conv_w[l]), writes=[prm])
        P.dma('sp', dng[:], bcast_part(g.dn_norm_g[l]), writes=[prm])
        P.dma('sp', dtb[:], bcast_part(g.dn_dt_bias[l]), writes=[prm])
        P.dma('sp', nea[:], bcast_part(g.dn_a_log[l]), writes=[prm])
        P.I('act', 'activation', out=nea[:], in_=nea[:], func=AF.Exp, reads=[prm], writes=[prm])
        P.I('dve', 'tensor_scalar', out=nea[:], in0=nea[:], scalar1=-1.0, scalar2=None, op0=ALU.mult, reads=[prm], writes=[prm])
        OF = P.sb('OF', [128, NTILE, 512], F32, ph)
        OFb = [P.buf() for _ in range(NTILE)]
        S = P.sb('S', [128, 2, 4, 128], F32, ph)
        Sb = [[P.buf() for _ in range(4)] for _ in range(2)]
        P.I('pool', 'memset', S[:], 0.0, writes=Sb[0] + Sb[1])
        xsr = Ring(P, 'xs', 5, [128, 1536], F32, ph)
        qkvr = Ring(P, 'qkv', 2, [128, 1536], F32, ph)
        sq8 = Ring(P, 'sq8', 1, [128, 1024], F32, ph)
        zr = Ring(P, 'z', 2, [128, 528], F32, ph)
        sm = Ring(P, 'sm', 16, [128, 16], F32, ph)
        m128 = Ring(P, 'm128', 24, [128, 128], F32, ph)
        m256 = Ring(P, 'm256', 12, [128, 256], F32, ph)
        e1 = Ring(P, 'e1', 4, [128, 1], F32, ph)
        osr = Ring(P, 'os', 2, [128, 512], F32, ph)
        sqr = Ring(P, 'sq', 2, [128, 512], F32, ph)
        sor = Ring(P, 'so', 2, [128, 512], F32, ph)
        QKVNb = [P.buf() for _ in range(NTILE)]
        pk = [0]

        def psn():
            r = g.ps[pk[0] % 7]
            pk[0] += 1
            return r
        ev = [0]

        def evac(out_ap, in_ap, reads, writes):
            if ev[0] % 2 == 0:
                P.I('act', 'activation', out=out_ap, in_=in_ap, func=AF.Copy, reads=reads, writes=writes)
            else:
                P.I('dve', 'tensor_copy', out=out_ap, in_=in_ap, reads=reads, writes=writes)
            ev[0] += 1
        for d in range(2):
            TRI = g.cst[:, (C_TRIF if d == 0 else C_TRIB):(C_TRIF if d == 0 else C_TRIB) + 128]
            GAM = g.cst[:, (C_GAF if d == 0 else C_GAB):(C_GAF if d == 0 else C_GAB) + 128]
            GPM = g.cst[:, (C_GPF if d == 0 else C_GPB):(C_GPF if d == 0 else C_GPB) + 128]
            tl = 127 if d == 0 else 0
            for c in chunk_order(d):
                if DEBUG_CTX_ONLY and c >= 2:
                    continue
                r0 = dn_rows(c)
                qkv, qkvb = qkvr.next()
                z, zb = zr.next()
                P.dma('sp', z[:], g.PJD[r0:r0 + 128, 1536:2064], writes=[zb])
                if d == 0:
                    xs = []
                    for j in range(4):
                        x_t, x_b = xsr.next()
                        P.dma('sp', x_t[:], g.PJD[r0 + j - 2:r0 + j - 2 + 128, 0:1536], writes=[x_b])
                        xs.append((x_t, x_b))
                    acc, accb = xsr.next()
                    P.I('dve', 'tensor_tensor', out=acc[:], in0=xs[0][0][:], in1=cwb[:, 0:1536], op=ALU.mult, reads=[xs[0][1], prm], writes=[accb])
                    for j in range(1, 4):
                        P.I('pool', 'tensor_tensor', out=xs[j][0][:], in0=xs[j][0][:], in1=cwb[:, j * 1536:(j + 1) * 1536], op=ALU.mult,
                            reads=[xs[j][1], prm], writes=[xs[j][1]])
                        P.I('dve', 'tensor_tensor', out=acc[:], in0=acc[:], in1=xs[j][0][:], op=ALU.add, reads=[accb, xs[j][1]], writes=[accb])
                    P.I('act', 'activation', out=qkv[:], in_=acc[:], func=AF.Silu, reads=[accb], writes=[qkvb])
                    sq, sqb = sq8.next()
                    ss, ssb = sm.next()
                    P.I('pool', 'tensor_tensor', out=sq[:], in0=qkv[:, 0:1024], in1=qkv[:, 0:1024], op=ALU.mult, reads=[qkvb], writes=[sqb])
                    P.I('dve', 'tensor_reduce', out=ss[:, 0:8], in_=sq[:].rearrange("p (h e) -> p h e", h=8), axis=mybir.AxisListType.X,
                        op=ALU.add, reads=[sqb], writes=[ssb])
                    P.I('dve', 'tensor_scalar', out=ss[:, 0:8], in0=ss[:, 0:8], scalar1=EPS, scalar2=None, op0=ALU.add, reads=[ssb], writes=[ssb])
                    P.I('pool', 'tensor_tensor', out=ss[:, 8:16], in0=ss[:, 0:8], in1=bcast_free(g.negh[:], 8), op=ALU.pow,
                        reads=[ssb, g.neghb], writes=[ssb])
                    P.I('dve', 'tensor_scalar', out=ss[:, 8:12], in0=ss[:, 8:12], scalar1=128 ** -0.5, scalar2=None, op0=ALU.mult,
                        reads=[ssb], writes=[ssb])
                    P.I('dve', 'tensor_tensor', out=qkv[:, 0:1024].rearrange("p (h e) -> p h e", h=8),
                        in0=qkv[:, 0:1024].rearrange("p (h e) -> p h e", h=8), in1=bc3(ss[:, 8:16], 128), op=ALU.mult,
                        reads=[qkvb, ssb], writes=[qkvb])
                    P.dma('sp', g.QKVN[c * 128:(c + 1) * 128, :], qkv[:], reads=[qkvb], writes=[QKVNb[c]])
                else:
                    P.dma('sp', qkv[:], g.QKVN[c * 128:(c + 1) * 128, :], reads=[QKVNb[c]], writes=[qkvb])
                beta, betab = sm.next()
                gd, gdb = sm.next()
                P.I('act', 'activation', out=beta[:, 0:4], in_=z[:, 512 + 4 * d:516 + 4 * d], func=AF.Sigmoid, reads=[zb], writes=[betab])
                P.I('dve', 'tensor_tensor', out=gd[:, 0:4], in0=z[:, 520 + 4 * d:524 + 4 * d], in1=dtb[:, 4 * d:4 * d + 4], op=ALU.add,
                    reads=[zb, prm], writes=[gdb])
                P.I('act', 'activation', out=gd[:, 0:4], in_=gd[:, 0:4], func=AF.Exp, reads=[gdb], writes=[gdb])
                P.I('act', 'activation', out=gd[:, 0:4], in_=gd[:, 0:4], func=AF.Ln, bias=1.0, reads=[gdb], writes=[gdb])
                P.I('dve', 'tensor_tensor', out=gd[:, 0:4], in0=gd[:, 0:4], in1=nea[:, 4 * d:4 * d + 4], op=ALU.mult, reads=[gdb, prm], writes=[gdb])
                pt, pb = psn()
                P.mm(pt[:, 0:4], TRI, gd[:, 0:4], True, True, reads=[g.cstb, gdb], writes=[pb])
                P.I('dve', 'tensor_copy', out=gd[:, 4:8], in_=pt[:, 0:4], reads=[pb], writes=[gdb])
                P.I('dve', 'tensor_scalar', out=gd[:, 8:12], in0=gd[:, 4:8], scalar1=-1.0, scalar2=None, op0=ALU.mult, reads=[gdb], writes=[gdb])
                P.I('act', 'activation', out=gd[:, 12:16], in_=gd[:, 4:8], func=AF.Exp, reads=[gdb], writes=[gdb])
                P.I('dve', 'tensor_scalar', out=beta[:, 4:8], in0=beta[:, 0:4], scalar1=-1.0, scalar2=None, op0=ALU.mult, reads=[betab], writes=[betab])
                P.I('dve', 'tensor_tensor', out=beta[:, 8:12], in0=beta[:, 0:4], in1=gd[:, 12:16], op=ALU.mult, reads=[betab, gdb], writes=[betab])
                if d == 1:
                    os_, osb = osr.next()
                for h in range(4):
                    Q = qkv[:, h * 128:(h + 1) * 128]
                    Kk = qkv[:, 512 + h * 128:512 + (h + 1) * 128]
                    V = qkv[:, 1024 + h * 128:1024 + (h + 1) * 128]
                    pt, pb = psn()
                    P.I('pe', 'transpose', pt[:, 0:128], Kk, g.ident, reads=[qkvb, g.cstb], writes=[pb])
                    P.I('pe', 'transpose', pt[:, 128:256], Q, g.ident, reads=[qkvb, g.cstb], writes=[pb])
                    KQT, KQTb = m256.next()
                    evac(KQT[:], pt[:, 0:256], [pb], [KQTb])
                    KT = KQT[:, 0:128]
                    QT = KQT[:, 128:256]
                    pkk, pkkb = psn()
                    P.mm(pkk[:, 0:128], KT, KT, True, True, reads=[KQTb], writes=[pkkb])
                    P.mm(pkk[:, 128:256], KT, QT, True, True, reads=[KQTb], writes=[pkkb])
                    pga, pgab = psn()
                    gcol = bcast_free(gd[:, h:h + 1], 128)
                    P.mm(pga[:, 0:128], gcol, TRI, True, False, reads=[gdb, g.cstb], writes=[pgab])
                    P.mm(pga[:, 0:128], g.ident, GAM, False, True, reads=[g.cstb], writes=[pgab])
                    P.mm(pga[:, 128:256], gcol, TRI, True, False, reads=[gdb, g.cstb], writes=[pgab])
                    P.mm(pga[:, 128:256], g.ident, GPM, False, True, reads=[g.cstb], writes=[pgab])
                    gamA, gamAb = m128.next()
                    gamP, gamPb = m128.next()
                    eGl, eGlb = e1.next()
                    P.I('act', 'activation', out=gamA[:], in_=pga[:, 0:128], func=AF.Exp, scale=-1.0, bias=gd[:, 4 + h:5 + h],
                        reads=[pgab, gdb], writes=[gamAb])
                    P.I('act', 'activation', out=gamP[:], in_=pga[:, 128:256], func=AF.Exp, bias=gd[:, 8 + h:9 + h],
                        reads=[pgab, gdb], writes=[gamPb])
                    P.I('act', 'activation', out=eGl[:], in_=pga[:, 128 + tl:129 + tl], func=AF.Exp, reads=[pgab], writes=[eGlb])
                    Pn0, Pn0b = m256.next()
                    PT, PTb = m128.next()
                    P.I('dve', 'scalar_tensor_tensor', out=Pn0[:, 0:128], in0=pkk[:, 0:128], scalar=beta[:, 4 + h:5 + h], in1=gamA[:],
                        op0=ALU.mult, op1=ALU.mult, reads=[pkkb, betab, gamAb], writes=[Pn0b])
                    P.I('dve', 'tensor_tensor', out=PT[:], in0=pkk[:, 128:256], in1=gamP[:], op=ALU.mult, reads=[pkkb, gamPb], writes=[PTb])
                    if DN_STOP_AT == 1:
                        continue
                    pt, pb = psn()
                    P.I('pe', 'transpose', pt[:, 0:128], Pn0[:, 0:128], g.ident, reads=[Pn0b, g.cstb], writes=[pb])
                    evac(Pn0[:, 128:256], pt[:, 0:128], [pb], [Pn0b])
                    BD = g.cst[:, C_BD:C_BD + 128]
                    Pbd, Pbdb = m256.next()
                    NToff, NToffb = m128.next()
                    P.I('pool', 'tensor_tensor', out=Pbd[:, 0:128], in0=Pn0[:, 0:128], in1=BD, op=ALU.mult, reads=[Pn0b, g.cstb], writes=[Pbdb])
                    P.I('pool', 'tensor_tensor', out=Pbd[:, 128:256], in0=Pn0[:, 128:256], in1=BD, op=ALU.mult, reads=[Pn0b, g.cstb], writes=[Pbdb])
                    P.I('pool', 'tensor_tensor', out=NToff[:], in0=Pn0[:, 128:256], in1=g.cst[:, C_OFF:C_OFF + 128], op=ALU.mult,
                        reads=[Pn0b, g.cstb], writes=[NToffb])
                    if DN_STOP_AT == 2:
                        continue
                    XT, XTb = m128.next()
                    P.I('pool', 'tensor_tensor', out=XT[:], in0=Pbd[:, 128:256], in1=g.ident, op=ALU.add, reads=[Pbdb, g.cstb], writes=[XTb])
                    Pj, Pjb = Pbd, Pbdb
                    for j in range(5):
                        pp, ppb = psn()
                        P.mm(pp[:, 0:128], Pj[:, 128:256], Pj[:, 0:128], True, True, reads=[Pjb], writes=[ppb])
                        w = 128
                        if j < 4:
                            P.mm(pp[:, 128:256], Pj[:, 0:128], Pj[:, 128:256], True, True, reads=[Pjb], writes=[ppb])
                            w = 256
                        Pn, Pnb = m256.next()
                        evac(Pn[:, 0:w], pp[:, 0:w], [ppb], [Pnb])
                        px, pxb = psn()
                        P.mm(px[:, 0:128], Pn[:, 0:128], XT[:], True, True, reads=[Pnb, XTb], writes=[pxb])
                        XTn, XTnb = m128.next()
                        P.I('dve', 'tensor_tensor', out=XTn[:], in0=px[:, 0:128], in1=XT[:], op=ALU.add, reads=[pxb, XTb], writes=[XTnb])
                        XT, XTb = XTn, XTnb
                        Pj, Pjb = Pn, Pnb
                    if DN_STOP_AT == 3:
                        continue
                    rhs, rhsb = m256.next()
                    P.I('pool', 'tensor_scalar', out=rhs[:, 0:128], in0=V, scalar1=beta[:, h:h + 1], scalar2=None, op0=ALU.mult,
                        reads=[qkvb, betab], writes=[rhsb])
                    P.I('pool', 'tensor_scalar', out=rhs[:, 128:256], in0=Kk, scalar1=beta[:, 8 + h:9 + h], scalar2=None, op0=ALU.mult,
                        reads=[qkvb, betab], writes=[rhsb])
                    p1, p1b = psn()
                    P.mm(p1[:, 0:256], XT[:], rhs[:], True, True, reads=[XTb, rhsb], writes=[p1b])
                    y1, y1b = m256.next()
                    evac(y1[:], p1[:, 0:256], [p1b], [y1b])
                    p2, p2b = psn()
                    P.mm(p2[:, 0:256], NToff[:], y1[:], True, True, reads=[NToffb, y1b], writes=[p2b])
                    y2, y2b = m256.next()
                    evac(y2[:], p2[:, 0:256], [p2b], [y2b])
                    puw, puwb = psn()
                    P.mm(puw[:, 0:256], XT[:], y2[:], True, True, reads=[XTb, y2b], writes=[puwb])
                    UW, UWb = m256.next()
                    P.I('dve', 'tensor_tensor', out=UW[:, 0:128], in0=puw[:, 0:128], in1=y1[:, 0:128], op=ALU.add,
                        reads=[puwb, y1b], writes=[UWb])
                    P.I('dve', 'scalar_tensor_tensor', out=UW[:, 128:256], in0=puw[:, 128:256], scalar=-1.0, in1=y1[:, 128:256],
                        op0=ALU.mult, op1=ALU.subtract, reads=[puwb, y1b], writes=[UWb])
                    if DN_STOP_AT == 4:
                        continue
                    Kt, Ktb = m128.next()
                    Qg, Qgb = m128.next()
                    P.I(KQ_ENG, 'tensor_scalar', out=Kt[:], in0=Kk, scalar1=gamP[:, tl:tl + 1], scalar2=None, op0=ALU.mult,
                        reads=[qkvb, gamPb], writes=[Ktb])
                    P.I(KQ_ENG, 'tensor_scalar', out=Qg[:], in0=Q, scalar1=gd[:, 12 + h:13 + h], scalar2=None, op0=ALU.mult,
                        reads=[qkvb, gdb], writes=[Qgb])
                    if DN_SUB == 1:
                        continue
                    pq, pqb = psn()
                    P.mm(pq[:, 0:128], Qg[:], g.ident, True, False, reads=[Qgb, g.cstb], writes=[pqb])
                    P.mm(pq[:, 0:128], UW[:, 128:256], PT[:], False, True, reads=[UWb, PTb], writes=[pqb])
                    if DN_SUB == 2:
                        continue
                    P.mm(pq[:, 128:256], UW[:, 128:256], Kt[:], True, True, reads=[UWb, Ktb], writes=[pqb])
                    if DN_SUB == 3:
                        continue
                    P.mm(pq[:, 256:384], Kt[:], UW[:, 0:128], True, True, reads=[UWb, Ktb], writes=[pqb])
                    if DN_SUB == 4:
                        continue
                    QtT, QtTb = m128.next()
                    MT, MTb = m128.next()
                    Bm, Bmb = m128.next()
                    P.I('act', 'activation', out=QtT[:], in_=pq[:, 0:128], func=AF.Copy, reads=[pqb], writes=[QtTb])
                    if DN_SUB == 5:
                        continue
                    P.I('dve', 'scalar_tensor_tensor', out=MT[:], in0=g.ident, scalar=eGl[:], in1=pq[:, 128:256], op0=ALU.mult, op1=ALU.add,
                        reads=[g.cstb, eGlb, pqb, QtTb], writes=[MTb])
                    if DN_SUB == 6:
                        continue
                    P.I('act', 'activation', out=Bm[:], in_=pq[:, 256:384], func=AF.Copy, reads=[pqb], writes=[Bmb])
                    if DN_STOP_AT == 5:
                        continue
                    po, pob = psn()
                    P.mm(po[:, 0:128], PT[:], UW[:, 0:128], True, False, reads=[PTb, UWb], writes=[pob])
                    P.mm(po[:, 0:128], QtT[:], S[:, d, h, :], False, True, reads=[QtTb, Sb[d][h]], writes=[pob])
                    P.mm(po[:, 128:256], MT[:], S[:, d, h, :], True, True, reads=[MTb, Sb[d][h]], writes=[pob])
                    if d == 0:
                        P.I('act', 'activation', out=OF[:, c, h * 128:(h + 1) * 128], in_=po[:, 0:128], func=AF.Copy,
                            reads=[pob], writes=[OFb[c]])
                    else:
                        P.I('dve', 'tensor_tensor', out=os_[:, h * 128:(h + 1) * 128], in0=po[:, 0:128], in1=OF[:, c, h * 128:(h + 1) * 128],
                            op=ALU.add, reads=[pob, OFb[c]], writes=[osb])
                    P.I('dve', 'tensor_tensor', out=S[:, d, h, :], in0=po[:, 128:256], in1=Bm[:], op=ALU.add,
                        reads=[pob, Bmb, Sb[d][h]], writes=[Sb[d][h]])
                if d == 1:
                    finalize_heads(g, os_, osb, dng[:], prm, z[:, 0:512], zb, AF.Silu, sqr, sor, sm, ydn_dsts(g, c), gn3=True)


def phase_merge(g, l, last):
    P = g.P
    with contextlib.ExitStack() as ph:
        wbr = P.sb('wbr', [128, 12, D], BF16, ph)
        wout = P.sb('wout', [128, 8, D], BF16, ph)
        wbrb, woutb = P.buf(), P.buf()
        for i in range(3):
            P.dma('sp', wbr[:, i * 4:(i + 1) * 4, :], g.WBR[l][i * 512:(i + 1) * 512, :].rearrange("(kc p) n -> p kc n", p=128),
                  reads=[g.WBRb[l]], writes=[wbrb])
        for i in range(2):
            P.dma('sp', wout[:, i * 4:(i + 1) * 4, :], g.WOUT[l][i * 512:(i + 1) * 512, :].rearrange("(kc p) n -> p kc n", p=128),
                  reads=[g.WOUTb[l]], writes=[woutb])
        mgg = P.sb('mgg', [128, D], F32, ph)
        modb = P.buf()
        ytr = Ring(P, 'yt', 4, [128, 512], F32, ph)
        yT = P.sb('yT', [128, 12, 512], BF16, ph)
        yTb = P.buf()
        ylr = Ring(P, 'yl', 2, [128, 512], F32, ph)
        sgr = Ring(P, 'sgm', 6, [128, 512], BF16, ph)
        m1r = Ring(P, 'm1', 3, [128, 512], F32, ph)
        tmr = Ring(P, 'tm', 3, [128, 512], F32, ph)
        mT = P.sb('mT', [128, 8, 512], BF16, ph)
        mTb = P.buf()
        xring = Ring(P, 'xt', 4, [128, D], F32, ph)
        yring = Ring(P, 'ysb', 2, [128, D], F32, ph)
        junk = (P.sb('junk', [128, D], BF16, ph), P.buf())
        stats = Ring(P, 'st', 16, [128, 1], F32, ph)
        blocks = BLOCKS[1:] if last else BLOCKS
        cur_set = None
        pk = 0
        for (row0, nt, s) in blocks:
            NT = nt * 128
            if s != cur_set:
                cur_set = s
                P.dma('sp', mgg[:], g.MODB[s, 5], reads=[g.MODBb[s][5]], writes=[modb])
            for t in range(nt):
                r = row0 + t * 128
                for br, src in ((0, g.YML), (2, g.YDN)):
                    yt, ytb = ytr.next()
                    P.dma('sp', yt[:], src[r:r + 128, :], writes=[ytb])
                    pt, pb = g.ps[pk % 7]
                    pk += 1
                    for kc in range(4):
                        P.I('pe', 'transpose', pt[:, kc * 128:(kc + 1) * 128], yt[:, kc * 128:(kc + 1) * 128], g.ident,
                            reads=[ytb, g.cstb], writes=[pb])
                    P.I('act', 'activation', out=yT[:, br * 4:(br + 1) * 4, t * 128:(t + 1) * 128],
                        in_=pt[:, 0:512].rearrange("p (k t) -> p k t", k=4), func=AF.Copy, reads=[pb], writes=[yTb])
            for kc in range(4):
                yl, ylb = ylr.next()
                P.dma('sp', yl[:, :NT], g.YLT[kc * 128:(kc + 1) * 128, row0:row0 + NT], writes=[ylb])
                P.I('pool', 'tensor_copy', out=yT[:, 4 + kc, :NT], in_=yl[:, :NT], reads=[ylb], writes=[yTb])
            for dc in range(8):
                m1, m1b = m1r.next()
                for n in range(3):
                    sg, sgb = sgr.next()
                    rr = n * D + dc * 128
                    P.dma('sp', sg[:, :NT], g.SGT[rr:rr + 128, row0:row0 + NT], writes=[sgb])
                    pt, pb = g.ps[pk % 7]
                    pk += 1
                    for kc in range(4):
                        P.mm(pt[:, :NT], wbr[:, n * 4 + kc, dc * 128:(dc + 1) * 128], yT[:, n * 4 + kc, :NT], kc == 0, kc == 3,
                             reads=[wbrb, yTb], writes=[pb])
                    if n == 0:
                        P.I('dve', 'tensor_tensor', out=m1[:, :NT], in0=pt[:, :NT], in1=sg[:, :NT], op=ALU.mult, reads=[pb, sgb], writes=[m1b])
                    else:
                        tmp_t, tmp_b = tmr.next()
                        P.I('dve', 'tensor_tensor', out=tmp_t[:, :NT], in0=pt[:, :NT], in1=sg[:, :NT], op=ALU.mult, reads=[pb, sgb], writes=[tmp_b])
                        if n == 1:
                            P.I('pool', 'tensor_tensor', out=m1[:, :NT], in0=m1[:, :NT], in1=tmp_t[:, :NT], op=ALU.add,
                                reads=[m1b, tmp_b], writes=[m1b])
                        else:
                            P.I('pool', 'tensor_tensor', out=mT[:, dc, :NT], in0=m1[:, :NT], in1=tmp_t[:, :NT], op=ALU.add,
                                reads=[m1b, tmp_b], writes=[mTb])
            for t in range(nt):
                r = row0 + t * 128
                x_t, x_b = xring.next()
                P.dma('sp', x_t[:], g.XS[r:r + 128, :], reads=[g.XSb[r // 128]], writes=[x_b])
                ysb, ysbb = yring.next()
                for cb in range(2):
                    py, pyb = g.ps[pk % 7]
                    pk += 1
                    for kc in range(8):
                        P.mm(py[:], mT[:, kc, t * 128:(t + 1) * 128], wout[:, kc, cb * 512:(cb + 1) * 512], kc == 0, kc == 7,
                             reads=[mTb, woutb], writes=[pyb])
                    P.I('act', 'activation', out=ysb[:, cb * 512:(cb + 1) * 512], in_=py[:], func=AF.Copy, reads=[pyb], writes=[ysbb])
                rs, rsb = rms_rstd(g, ysb[:], [ysbb], junk[0][:], junk[1], stats)
                P.I('dve', 'scalar_tensor_tensor', out=ysb[:], in0=ysb[:], scalar=rs[:], in1=mgg[:], op0=ALU.mult, op1=ALU.mult,
                    reads=[ysbb, rsb, modb], writes=[ysbb])
                P.I('pool', 'tensor_tensor', out=x_t[:], in0=x_t[:], in1=ysb[:], op=ALU.add, reads=[ysbb, x_b], writes=[x_b])
                P.dma('sp', g.XS[r:r + 128, :], x_t[:], reads=[x_b], writes=[g.XSb[r // 128]])


_CACHE = {}


def kernel(**inputs):
    if 'nc' not in _CACHE:
        _CACHE['nc'] = build()
    nc = _CACHE['nc']
    f32 = lambda a: np.ascontiguousarray(np.asarray(a, dtype=np.float32))
    shared = dict(
        c_ctx=f32(inputs['c_ctx']), w_mod=f32(inputs['w_mod']), b_mod=f32(inputs['b_mod']),
        norm_g=f32(inputs['norm_g']).reshape(DEPTH, 6 * D), ffn_w_gu=f32(inputs['ffn_w_gu']),
        ffn_w_down=f32(inputs['ffn_w_down']), w_in=f32(inputs['w_in']),
        ml_gate_b=f32(inputs['ml_gate_b']).reshape(DEPTH, 16), ml_norm_g=f32(inputs['ml_norm_g']),
        lru_conv_w=f32(inputs['lru_conv_w']), lru_conv_b=f32(inputs['lru_conv_b']),
        lru_w_a=f32(inputs['lru_w_a']), lru_b_a=f32(inputs['lru_b_a']), lru_w_x=f32(inputs['lru_w_x']),
        lru_b_x=f32(inputs['lru_b_x']), lru_lambda=f32(inputs['lru_lambda']),
        dn_conv_w=f32(inputs['dn_conv_w']).reshape(DEPTH, 4 * 1536), dn_a_log=f32(inputs['dn_a_log']).reshape(DEPTH, 8),
        dn_dt_bias=f32(inputs['dn_dt_bias']).reshape(DEPTH, 8), dn_norm_g=f32(inputs['dn_norm_g']),
        w_branch=f32(inputs['w_branch']).reshape(DEPTH, 1536, D), w_out=f32(inputs['w_out']),
        consts=make_consts())
    x = f32(inputs['x'])
    c = f32(inputs['c'])
    ctx = f32(inputs['ctx'])
    B = x.shape[0]
    in_maps = [dict(shared, x=x[b], c=c[b], ctx=ctx[b]) for b in range(B)]
    res = run_bass_kernel_spmd(nc, in_maps, core_ids=list(range(B)))
    return np.stack([np.asarray(r['out'], dtype=np.float32) for r in res.results], axis=0)
```
